# Optimizing a Trainium2 kernel written in Bass

```python
import jax, jax.numpy as jnp
from jax import lax
import numpy as np


D_MODEL = 1024
BATCH = 32
SEQ = 2048
DEPTH = 2

GRID_W = 64
CTX_LEN = 256
N_MIXERS = 2
N_ATT_LAYERS = (DEPTH + 1) // 2
N_MLSTM_LAYERS = DEPTH // 2
ATT_HEADS = 8
ATT_KV_HEADS = 2
ATT_GROUP = ATT_HEADS // ATT_KV_HEADS
ATT_HEAD_DIM = D_MODEL // ATT_HEADS
ROPE_PAIRS_PER_AXIS = ATT_HEAD_DIM // 4
ROPE_THETA = 10000.0
Q_BLOCK = 128
MLSTM_HEADS = 4
MLSTM_V_DIM = D_MODEL // MLSTM_HEADS
MLSTM_QK_DIM = MLSTM_V_DIM // 2
MLSTM_CONV_W = 3
MLSTM_CHUNK = 64
D_FF = 2816
N_MOD = 9
DEEPNORM_ALPHA = (2.0 * DEPTH) ** 0.25
DEEPNORM_BETA = (8.0 * DEPTH) ** -0.25
LN_EPS = 1e-5
RMS_EPS = 1e-6

kernel_name = "hybrid_gqa_mlstm_macaron_dit"


def layer_norm(x, g, b):
    xf = x.astype(jnp.float32)
    mu = jnp.mean(xf, axis=-1, keepdims=True)
    var = jnp.mean(jnp.square(xf - mu), axis=-1, keepdims=True)
    return ((xf - mu) * lax.rsqrt(var + LN_EPS) * g + b).astype(x.dtype)


def rms_norm(x, g):
    xf = x.astype(jnp.float32)
    return (xf * lax.rsqrt(jnp.mean(jnp.square(xf), axis=-1, keepdims=True) + RMS_EPS) * g).astype(x.dtype)


def modulate(h, shift, scale):
    return h * (1.0 + scale) + shift


def post_norm(h, delta, g, b):
    return layer_norm(DEEPNORM_ALPHA * h + delta, g, b)


def half_ffn(h, shift, scale, gate, w_in, w_out, g, b):
    xm = modulate(h, shift, scale)
    a, u = jnp.split(xm @ w_in, 2, axis=-1)
    y = (jax.nn.silu(a) * u) @ w_out
    return post_norm(h, 0.5 * gate * y, g, b)


def axial_rope_tables(n_tokens):
    rows = n_tokens // GRID_W
    row = jnp.repeat(jnp.arange(rows, dtype=jnp.int32), GRID_W).astype(jnp.float32)
    col = jnp.tile(jnp.arange(GRID_W, dtype=jnp.int32), rows).astype(jnp.float32)
    inv = ROPE_THETA ** (-jnp.arange(ROPE_PAIRS_PER_AXIS, dtype=jnp.float32) / ROPE_PAIRS_PER_AXIS)
    ang = jnp.stack([row[:, None] * inv, col[:, None] * inv], axis=1)
    return jnp.cos(ang), jnp.sin(ang)


def apply_rope(x, cos, sin):
    B, L, H, Dh = x.shape
    xr = x.reshape(B, L, H, 2, 2, ROPE_PAIRS_PER_AXIS)
    x1, x2 = xr[..., 0, :], xr[..., 1, :]
    c = cos[None, :, None]
    s = sin[None, :, None]
    out = jnp.stack([x1 * c - x2 * s, x2 * c + x1 * s], axis=-2)
    return out.reshape(B, L, H, Dh).astype(x.dtype)


def gqa_softmax(q, k, v):
    s = jnp.einsum('bqhgd,bkhd->bhgqk', q, k).astype(jnp.float32) * (ATT_HEAD_DIM ** -0.5)
    p = jax.nn.softmax(s, axis=-1).astype(v.dtype)
    return jnp.einsum('bhgqk,bkhd->bqhgd', p, v)


def attention_mixer(xl, xc, cos, sin, w_in, q_gain, k_gain, w_out, need_ctx):
    def proj(xs):
        B, L, _ = xs.shape
        p = xs @ w_in
        q, k, v = jnp.split(p, [ATT_HEADS * ATT_HEAD_DIM, (ATT_HEADS + ATT_KV_HEADS) * ATT_HEAD_DIM], axis=-1)
        q = rms_norm(q.reshape(B, L, ATT_HEADS, ATT_HEAD_DIM), q_gain)
        k = rms_norm(k.reshape(B, L, ATT_KV_HEADS, ATT_HEAD_DIM), k_gain)
        v = v.reshape(B, L, ATT_KV_HEADS, ATT_HEAD_DIM)
        return q, k, v

    ql, kl, vl = proj(xl)
    qc, kc, vc = proj(xc)
    ql = apply_rope(ql, cos, sin)
    kl = apply_rope(kl, cos, sin)
    B, L = xl.shape[0], xl.shape[1]
    k_all = jnp.concatenate([kl, kc], axis=1)
    v_all = jnp.concatenate([vl, vc], axis=1)
    nb = L // Q_BLOCK
    qb = jnp.moveaxis(ql.reshape(B, nb, Q_BLOCK, ATT_KV_HEADS, ATT_GROUP, ATT_HEAD_DIM), 1, 0)
    ob = lax.map(lambda qblk: gqa_softmax(qblk, k_all, v_all), qb)
    o_lat = jnp.moveaxis(ob, 0, 1).reshape(B, L, D_MODEL) @ w_out
    o_ctx = None
    if need_ctx:
        Lc = xc.shape[1]
        oc = gqa_softmax(qc.reshape(B, Lc, ATT_KV_HEADS, ATT_GROUP, ATT_HEAD_DIM), kc, vc)
        o_ctx = oc.reshape(B, Lc, D_MODEL) @ w_out
    return o_lat, o_ctx


def centred_conv(a, w, b):
    L = a.shape[1]
    pad = MLSTM_CONV_W // 2
    ap = jnp.pad(a, ((0, 0), (pad, pad), (0, 0)))
    out = ap[:, 0:L] * w[0]
    for j in range(1, MLSTM_CONV_W):
        out = out + ap[:, j:j + L] * w[j]
    return out + b


def mlstm_scan(q, k, v, log_i, log_f, state):
    B, H, L, dk = q.shape
    dv = v.shape[-1]
    Lc = MLSTM_CHUNK
    nc = L // Lc

    def chunks(a):
        return jnp.moveaxis(a.reshape((B, H, nc, Lc) + a.shape[3:]), 2, 0)

    tri = jnp.tril(jnp.ones((Lc, Lc), dtype=bool))

    def step(carry, inp):
        C, n, m = carry
        qc, kc, vc, ic, fc = inp
        bcum = jnp.cumsum(fc, axis=-1)
        d = jnp.where(tri, bcum[..., :, None] - bcum[..., None, :] + ic[..., None, :], -jnp.inf)
        inter = m[..., None] + bcum
        m_t = jnp.maximum(inter, jnp.max(d, axis=-1))
        w_intra = jnp.exp(d - m_t[..., None])
        w_inter = jnp.exp(inter - m_t)
        s = jnp.einsum('bhtd,bhsd->bhts', qc, kc) * w_intra
        num = jnp.einsum('bhts,bhsv->bhtv', s, vc) + w_inter[..., None] * jnp.einsum('bhvd,bhtd->bhtv', C, qc)
        den = jnp.sum(s, axis=-1) + w_inter * jnp.einsum('bhd,bhtd->bht', n, qc)
        h = num / jnp.maximum(jnp.abs(den), jnp.exp(-m_t))[..., None]
        b_last = bcum[..., -1]
        d_last = b_last[..., None] - bcum + ic
        m_new = jnp.maximum(m + b_last, jnp.max(d_last, axis=-1))
        wk = jnp.exp(d_last - m_new[..., None])
        decay = jnp.exp(m + b_last - m_new)
        C_new = decay[..., None, None] * C + jnp.einsum('bhs,bhsv,bhsd->bhvd', wk, vc, kc)
        n_new = decay[..., None] * n + jnp.einsum('bhs,bhsd->bhd', wk, kc)
        return (C_new, n_new, m_new), h

    final, hs = lax.scan(step, state, (chunks(q), chunks(k), chunks(v), chunks(log_i), chunks(log_f)))
    h = jnp.moveaxis(hs, 0, 2).reshape(B, H, L, dv)
    return h, final


def mlstm_mixer(xl, xc, w_in, gate_b, conv_w, conv_b, norm_g, w_out, need_ctx):
    HQK = MLSTM_HEADS * MLSTM_QK_DIM
    HV = MLSTM_HEADS * MLSTM_V_DIM

    def proj(xs):
        B, L, _ = xs.shape
        p = xs @ w_in
        qk, v, o, g = jnp.split(p, [2 * HQK, 2 * HQK + HV, 2 * HQK + HV + D_MODEL], axis=-1)
        qk = jax.nn.silu(centred_conv(qk, conv_w, conv_b))
        q, k = jnp.split(qk, 2, axis=-1)

        def heads(a, dh):
            return jnp.transpose(a.reshape(B, L, MLSTM_HEADS, dh), (0, 2, 1, 3)).astype(jnp.float32)

        q = heads(q, MLSTM_QK_DIM)
        k = heads(k, MLSTM_QK_DIM) * (MLSTM_QK_DIM ** -0.5)
        v = heads(v, MLSTM_V_DIM)
        g = jnp.transpose((g.astype(jnp.float32) + gate_b).reshape(B, L, 4, MLSTM_HEADS), (2, 0, 3, 1))
        return q, k, v, g, o

    def zero_state(B):
        return (jnp.zeros((B, MLSTM_HEADS, MLSTM_V_DIM, MLSTM_QK_DIM), jnp.float32),
                jnp.zeros((B, MLSTM_HEADS, MLSTM_QK_DIM), jnp.float32),
                jnp.zeros((B, MLSTM_HEADS), jnp.float32))

    def flip(a):
        return jnp.flip(a, axis=2)

    def readout(h, o, dtype):
        B, H, L, dv = h.shape
        mu = jnp.mean(h, axis=-1, keepdims=True)
        var = jnp.mean(jnp.square(h - mu), axis=-1, keepdims=True)
        hn = (h - mu) * lax.rsqrt(var + LN_EPS)
        hn = jnp.transpose(hn, (0, 2, 1, 3)).reshape(B, L, D_MODEL) * norm_g
        return (hn * jax.nn.sigmoid(o.astype(jnp.float32))).astype(dtype) @ w_out

    ql, kl, vl, gl, ol = proj(xl)
    qc, kc, vc, gc, oc = proj(xc)
    B = xl.shape[0]
    hc_f, st_f = mlstm_scan(qc, kc, vc, gc[0], jax.nn.log_sigmoid(gc[1]), zero_state(B))
    hl_f, _ = mlstm_scan(ql, kl, vl, gl[0], jax.nn.log_sigmoid(gl[1]), st_f)
    hc_b, st_b = mlstm_scan(flip(qc), flip(kc), flip(vc), flip(gc[2]), jax.nn.log_sigmoid(flip(gc[3])), zero_state(B))
    hl_b, _ = mlstm_scan(flip(ql), flip(kl), flip(vl), flip(gl[2]), jax.nn.log_sigmoid(flip(gl[3])), st_b)
    o_lat = readout(hl_f + flip(hl_b), ol, xl.dtype)
    o_ctx = readout(hc_f + flip(hc_b), oc, xc.dtype) if need_ctx else None
    return o_lat, o_ctx


def setup_inputs(seed: int = 0) -> dict:
    key = jax.random.key(seed)
    ks = jax.random.split(key, 24)
    D = D_MODEL
    HQK = MLSTM_HEADS * MLSTM_QK_DIM
    HV = MLSTM_HEADS * MLSTM_V_DIM
    att_cols = (ATT_HEADS + 2 * ATT_KV_HEADS) * ATT_HEAD_DIM
    ml_cols = 2 * HQK + HV + D + 4 * MLSTM_HEADS

    def nrm(k, shape, scale):
        return jax.random.normal(k, shape, jnp.float32) * scale

    forget_b = jnp.linspace(3.0, 6.0, MLSTM_HEADS, dtype=jnp.float32)
    gnoise = nrm(ks[15], (N_MLSTM_LAYERS, 4, MLSTM_HEADS), 0.1)
    gate_b = gnoise + jnp.stack([jnp.zeros_like(forget_b), forget_b, jnp.zeros_like(forget_b), forget_b], axis=0)[None]
    return {
        "x": nrm(ks[0], (BATCH, SEQ, D), 1.0),
        "c": nrm(ks[1], (BATCH, D), 1.0),
        "ctx": nrm(ks[2], (BATCH, CTX_LEN, D), 1.0),
        "c_ctx": nrm(ks[3], (D,), 1.0),
        "ada_w": nrm(ks[4], (DEPTH, D, N_MOD * D), 0.5 * D ** -0.5),
        "ada_b": nrm(ks[5], (DEPTH, N_MOD * D), 0.02),
        "ln_g": 1.0 + nrm(ks[6], (DEPTH, 3, D), 0.02),
        "ln_b": nrm(ks[7], (DEPTH, 3, D), 0.02),
        "ffn_w_in": nrm(ks[8], (DEPTH, 2, D, 2 * D_FF), D ** -0.5),
        "ffn_w_out": nrm(ks[9], (DEPTH, 2, D_FF, D), DEEPNORM_BETA * D_FF ** -0.5),
        "att_w_in": nrm(ks[10], (N_ATT_LAYERS, D, att_cols), D ** -0.5),
        "att_q_gain": 1.0 + nrm(ks[11], (N_ATT_LAYERS, ATT_HEAD_DIM), 0.02),
        "att_k_gain": 1.0 + nrm(ks[12], (N_ATT_LAYERS, ATT_HEAD_DIM), 0.02),
        "att_w_out": nrm(ks[13], (N_ATT_LAYERS, D, D), DEEPNORM_BETA * D ** -0.5),
        "ml_w_in": nrm(ks[14], (N_MLSTM_LAYERS, D, ml_cols), D ** -0.5),
        "ml_gate_b": gate_b.reshape(N_MLSTM_LAYERS, 4 * MLSTM_HEADS),
        "ml_conv_w": nrm(ks[16], (N_MLSTM_LAYERS, MLSTM_CONV_W, 2 * HQK), MLSTM_CONV_W ** -0.5),
        "ml_conv_b": nrm(ks[17], (N_MLSTM_LAYERS, 2 * HQK), 0.02),
        "ml_norm_g": 1.0 + nrm(ks[18], (N_MLSTM_LAYERS, D), 0.02),
        "ml_w_out": nrm(ks[19], (N_MLSTM_LAYERS, D, D), DEEPNORM_BETA * D ** -0.5),
    }


def reference(x, c, ctx, c_ctx, ada_w, ada_b, ln_g, ln_b, ffn_w_in, ffn_w_out,
              att_w_in, att_q_gain, att_k_gain, att_w_out,
              ml_w_in, ml_gate_b, ml_conv_w, ml_conv_b, ml_norm_g, ml_w_out):
    n_lat = x.shape[1]
    cos, sin = axial_rope_tables(n_lat)
    h_lat, h_ctx = x, ctx
    for i in range(DEPTH):
        need_ctx = i < DEPTH - 1
        mod_lat = jnp.split((jax.nn.silu(c) @ ada_w[i] + ada_b[i])[:, None, :], N_MOD, axis=-1)
        mod_ctx = jnp.split((jax.nn.silu(c_ctx) @ ada_w[i] + ada_b[i])[None, None, :], N_MOD, axis=-1)
        h_lat = half_ffn(h_lat, mod_lat[0], mod_lat[1], mod_lat[2], ffn_w_in[i, 0], ffn_w_out[i, 0], ln_g[i, 0], ln_b[i, 0])
        h_ctx = half_ffn(h_ctx, mod_ctx[0], mod_ctx[1], mod_ctx[2], ffn_w_in[i, 0], ffn_w_out[i, 0], ln_g[i, 0], ln_b[i, 0])
        xm_lat = modulate(h_lat, mod_lat[3], mod_lat[4])
        xm_ctx = modulate(h_ctx, mod_ctx[3], mod_ctx[4])
        j = i // N_MIXERS
        if i % N_MIXERS == 0:
            o_lat, o_ctx = attention_mixer(xm_lat, xm_ctx, cos, sin, att_w_in[j], att_q_gain[j], att_k_gain[j],
                                           att_w_out[j], need_ctx)
        else:
            o_lat, o_ctx = mlstm_mixer(xm_lat, xm_ctx, ml_w_in[j], ml_gate_b[j], ml_conv_w[j], ml_conv_b[j],
                                       ml_norm_g[j], ml_w_out[j], need_ctx)
        h_lat = post_norm(h_lat, mod_lat[5] * o_lat, ln_g[i, 1], ln_b[i, 1])
        h_lat = half_ffn(h_lat, mod_lat[6], mod_lat[7], mod_lat[8], ffn_w_in[i, 1], ffn_w_out[i, 1], ln_g[i, 2], ln_b[i, 2])
        if need_ctx:
            h_ctx = post_norm(h_ctx, mod_ctx[5] * o_ctx, ln_g[i, 1], ln_b[i, 1])
            h_ctx = half_ffn(h_ctx, mod_ctx[6], mod_ctx[7], mod_ctx[8], ffn_w_in[i, 1], ffn_w_out[i, 1], ln_g[i, 2], ln_b[i, 2])
    return h_lat
```

```python
import numpy as np
from contextlib import ExitStack
import concourse.bass as bass
import concourse.mybir as mybir
from concourse.bass_utils import run_bass_kernel_spmd

F32 = mybir.dt.float32
BF16 = mybir.dt.bfloat16
AF = mybir.ActivationFunctionType
ALU = mybir.AluOpType


class Buf:
    __slots__ = ("t", "w", "r", "name")

    def __init__(self, t=None, name=""):
        self.t = t
        self.w = None
        self.r = {}
        self.name = name


class KB:
    ENGS = ("tensor", "vector", "scalar", "gpsimd", "sync")
    NDMA = 32
    EPOCH = 30000

    def __init__(self, nc):
        self.nc = nc
        self.stack = ExitStack()

    def __enter__(self):
        nc = self.nc
        self.stack.__enter__()
        self.semstack = self.stack
        self.old_tokens = []
        self.eng = {n: getattr(nc, n) for n in self.ENGS}
        self.nsem = 0
        self.esem = {}
        self.ecnt = {}
        self.waited = {n: {} for n in self.ENGS}
        for n in self.ENGS:
            self._new_esem(n)
        self.dsem = {"hw": [], "sw": []}
        for r in ("hw", "sw"):
            for i in range(self.NDMA):
                s = self.stack.enter_context(nc.semaphore("d%s%d" % (r, i)))
                self.dsem[r].append([self._key(), s, 0, None])
        self.dnext = {"hw": 0, "sw": 0}
        self.out_tokens = []
        self.ninstr = 0
        return self

    def __exit__(self, *a):
        return self.stack.__exit__(*a)

    def _key(self):
        self.nsem += 1
        return self.nsem

    def _new_esem(self, n):
        if n in self.esem and self.ecnt[n] > 0:
            k0, s0 = self.esem[n]
            self.old_tokens.append((k0, s0, self.ecnt[n], n))
        s = self.semstack.enter_context(self.nc.semaphore("e%s%d" % (n, self.nsem)))
        self.esem[n] = (self._key(), s)
        self.ecnt[n] = 0

    def sb(self, name, shape, dt):
        self.ntens = getattr(self, "ntens", 0) + 1
        t = self.stack.enter_context(self.nc.sbuf_tensor("sb%d_%s" % (self.ntens, name), list(shape), dt))
        return Buf(t, name)

    def ps(self, name, shape, dt=F32):
        self.ntens = getattr(self, "ntens", 0) + 1
        t = self.stack.enter_context(self.nc.psum_tensor("ps%d_%s" % (self.ntens, name), list(shape), dt))
        return Buf(t, name)

    def _wait(self, en, tok):
        if tok is None:
            return
        key, sem, val, src_en = tok
        if src_en == en == "tensor":
            return
        w = self.waited[en]
        if w.get(key, 0) >= val:
            return
        self.eng[en].wait_ge(sem, val)
        self.ninstr += 1
        w[key] = val

    def _deps(self, en, rd, wr):
        for b in rd:
            self._wait(en, b.w)
        for b in wr:
            self._wait(en, b.w)
            for e2, tok in b.r.items():
                if e2 != en:
                    self._wait(en, tok)

    def _mark(self, en, tok, rd, wr):
        for b in rd:
            b.r[en] = tok
        for b in wr:
            b.w = tok
            b.r = {}

    def op(self, en, method, *args, rd=(), wr=(), **kw):
        self._deps(en, rd, wr)
        ins = getattr(self.eng[en], method)(*args, **kw)
        key, sem = self.esem[en]
        ins.then_inc(sem, 1)
        self.ecnt[en] += 1
        self.ninstr += 1
        tok = (key, sem, self.ecnt[en], en)
        self._mark(en, tok, rd, wr)
        if self.ecnt[en] >= self.EPOCH:
            self._new_esem(en)
        return tok

    def dma(self, q, dst_buf, out_ap, in_ap, rd=(), out=False, **kw):
        if dst_buf is None:
            wr = []
        elif isinstance(dst_buf, (list, tuple)):
            wr = list(dst_buf)
        else:
            wr = [dst_buf]
        self._deps(q, rd, wr)
        ring = "sw" if q == "gpsimd" else "hw"
        d = self.dsem[ring][self.dnext[ring]]
        self.dnext[ring] = (self.dnext[ring] + 1) % self.NDMA
        self._wait(q, d[3])
        ins = self.eng[q].dma_start(out=out_ap, in_=in_ap, **kw)
        d[2] += 16
        ins.then_inc(d[1], 16)
        self.ninstr += 1
        tok = (d[0], d[1], d[2], "dma")
        d[3] = tok
        for b in rd:
            b.r["dma%d" % d[0]] = tok
        for b in wr:
            b.w = tok
            b.r = {}
        if out:
            self.out_tokens.append(tok)
        return tok


    def barrier(self):
        toks = []
        for n in self.ENGS:
            if self.ecnt[n] > 0:
                key, sem = self.esem[n]
                toks.append((key, sem, self.ecnt[n], n))
        toks += [d[3] for r in self.dsem.values() for d in r if d[3] is not None]
        toks += self.old_tokens
        for n in self.ENGS:
            for t in toks:
                self._wait(n, t)

    class _Scope:
        def __init__(self, kb):
            self.kb = kb
        def __enter__(self):
            self.saved = self.kb.stack
            self.kb.stack = ExitStack()
            self.kb.stack.__enter__()
            return self
        def __exit__(self, *a):
            if a[0] is None:
                self.kb.barrier()
            r = self.kb.stack.__exit__(*a)
            self.kb.stack = self.saved
            return r

    def scope(self):
        return KB._Scope(self)

    def finish(self):
        for tok in self.out_tokens:
            self._wait("sync", tok)
        for r in self.dsem.values():
            for d in r:
                self._wait("sync", d[3])


NCORE = 8
NB = 4
NCTX = 256
NLAT = 2048
TOK = NB * (NCTX + NLAT)
D = 1024
DFF = 2816
ALPHA = 4.0 ** 0.25
EPS_LN = 1e-5 / (ALPHA * ALPHA)
NBLK = 256


def ctx0(b):
    return b * NCTX


def lat0(b):
    return NB * NCTX + b * NLAT


def VC_ADAB(l):
    return l * 72


def VC_LNG(l, j):
    return 144 + (l * 3 + j) * 8


def VC_LNB(l, j):
    return 192 + (l * 3 + j) * 8


VC_QG = 240
VC_KG = 241
VC_CW = 242
VC_CB = 266
NVEC = 274


def all_blocks():
    blks = []
    for b in range(NB):
        blks.append((ctx0(b), 4))
    for b in range(NB):
        for i in range(NLAT // NBLK):
            blks.append((lat0(b) + i * NBLK, b))
    return blks


def fm(ap):
    return ap.rearrange("(k p) t -> p k t", p=128)


def prologue(kb, io, G):
    modsT, vecs = G["modsT"], G["vecs"]
    kb.dma("sync", vecs, vecs.t[:], io["vecs"][:, :])
    kb.op("gpsimd", "memset", G["ones_ln"].t[:], 1.0 / D, wr=[G["ones_ln"]])
    with kb.scope():
        cT = kb.sb("cT", [128, 8, 6], F32)
        scT = kb.sb("scT", [128, 8, 6], F32)
        kb.dma("sync", cT, cT.t[:], io["cT"][:, :, :])
        kb.op("scalar", "activation", scT.t[:], cT.t[:], AF.Silu, rd=[cT], wr=[scT])
        aw = [kb.sb("aw%d" % i, [128, 8, 1152], F32) for i in range(2)]
        ps = [kb.ps("pps%d" % i, [128, 512]) for i in range(2)]
        n = 0
        for l in range(2):
            awl = io["ada_w"][l].rearrange("(k p) n -> p k n", p=128)
            for piece in range(8):
                a = aw[n % 2]
                n += 1
                kb.dma("sync", a, a.t[:], awl[:, :, piece * 1152:(piece + 1) * 1152])
                for f in range(9):
                    fc = piece * 9 + f
                    p = ps[fc % 2]
                    for k in range(8):
                        kb.op("tensor", "matmul", p.t[:, 0:6], a.t[:, k, f * 128:(f + 1) * 128], scT.t[:, k, :],
                              start=(k == 0), stop=(k == 7), rd=[a, scT], wr=[p])
                    kb.op("vector", "tensor_scalar", modsT.t[:, l, fc, :], p.t[:, 0:6],
                          vecs.t[:, VC_ADAB(l) + fc:VC_ADAB(l) + fc + 1], None, ALU.add,
                          rd=[p, vecs], wr=[modsT])
        for l in range(2):
            for m in (1, 4, 7):
                kb.op("vector", "tensor_scalar", modsT.t[:, l, m * 8:(m + 1) * 8, :], modsT.t[:, l, m * 8:(m + 1) * 8, :],
                      1.0, None, ALU.add, rd=[modsT], wr=[modsT])
            for m, c in ((2, 0.5 / ALPHA), (8, 0.5 / ALPHA), (5, 1.0 / ALPHA)):
                kb.op("vector", "tensor_scalar", modsT.t[:, l, m * 8:(m + 1) * 8, :], modsT.t[:, l, m * 8:(m + 1) * 8, :],
                      c, None, ALU.mult, rd=[modsT], wr=[modsT])


def ln_tail(kb, G, hb, hbc, sq, sqb, T, N, l, lnj, pst):
    vecs, ones = G["vecs"], G["ones_ln"]
    pm, pq = pst
    if isinstance(pm, tuple):
        (pm, pma), (pq, pqa) = pm, pq
    else:
        pma, pqa = pm.t[:, 0:N], pq.t[:, 0:N]
    for m in range(8):
        kb.op("tensor", "matmul", pma, ones.t[:], hb.t[:, m, 0:N], start=(m == 0), stop=(m == 7),
              rd=[ones, hbc[m]], wr=[pm])
    for m in range(8):
        kb.op("tensor", "matmul", pqa, ones.t[:], sq.t[:, m, 0:N], start=(m == 0), stop=(m == 7),
              rd=[ones, sqb[m]], wr=[pq])
    mean, tmp, sd, rstd = T["mean"], T["tmp"], T["sd"], T["rstd"]
    kb.op("scalar", "activation", mean.t[:, 0:N], pma, AF.Identity, rd=[pm], wr=[mean])
    kb.op("vector", "tensor_tensor", tmp.t[:, 0:N], mean.t[:, 0:N], mean.t[:, 0:N], ALU.mult, rd=[mean], wr=[tmp])
    kb.op("vector", "tensor_tensor", tmp.t[:, 0:N], pqa, tmp.t[:, 0:N], ALU.subtract, rd=[pq, tmp], wr=[tmp])
    kb.op("vector", "tensor_scalar", tmp.t[:, 0:N], tmp.t[:, 0:N], EPS_LN, None, ALU.add, rd=[tmp], wr=[tmp])
    kb.op("scalar", "activation", sd.t[:, 0:N], tmp.t[:, 0:N], AF.Sqrt, rd=[tmp], wr=[sd])
    kb.op("vector", "reciprocal", rstd.t[:, 0:N], sd.t[:, 0:N], rd=[sd], wr=[rstd])
    for m in range(8):
        kb.op("vector", "tensor_tensor", sq.t[:, m, 0:N], hb.t[:, m, 0:N], mean.t[:, 0:N], ALU.subtract,
              rd=[hbc[m], mean], wr=[sqb[m]])
        kb.op("vector", "tensor_tensor", sq.t[:, m, 0:N], sq.t[:, m, 0:N], rstd.t[:, 0:N], ALU.mult,
              rd=[sqb[m], rstd], wr=[sqb[m]])
        cg = VC_LNG(l, lnj) + m
        cb = VC_LNB(l, lnj) + m
        kb.op("scalar", "activation", hb.t[:, m, 0:N], sq.t[:, m, 0:N], AF.Identity,
              bias=vecs.t[:, cb:cb + 1], scale=vecs.t[:, cg:cg + 1], rd=[sqb[m], vecs], wr=[hbc[m]])


def ffn_stage(kb, io, G, src, dst, l, j, blocks, dst_off=0, final=False):
    N = NBLK
    modsT, vecs = G["modsT"], G["vecs"]
    w_in = io["ffn_w_in"][l, j]
    w_out = io["ffn_w_out"][l, j]
    m_sh, m_sc, m_g = (0, 1, 2) if j == 0 else (6, 7, 8)
    lnj = 0 if j == 0 else 2
    srcv, dstv = fm(src), fm(dst)
    with kb.scope():
        win = [kb.sb("win%d" % k, [128, 2 * DFF], BF16) for k in range(8)]
        wout = [kb.sb("wout%d" % q, [128, D], BF16) for q in range(22)]
        for k in range(8):
            kb.dma("gpsimd", win[k], win[k].t[:], w_in[k * 128:(k + 1) * 128, :])
        for q in range(22):
            kb.dma("gpsimd", wout[q], wout[q].t[:], w_out[q * 128:(q + 1) * 128, :])
        hb = [kb.sb("hb%d" % i, [128, 8, N], F32) for i in range(3)]
        hbc = [[Buf() for _ in range(8)] for _ in range(3)]
        xm = [kb.sb("xm%d" % i, [128, 8, N], BF16) for i in range(2)]
        xmb = [[Buf() for _ in range(8)] for _ in range(2)]
        g = kb.sb("g", [128, 22, N], BF16)
        gb = [Buf() for _ in range(22)]
        sq = kb.sb("sq", [128, 8, N], F32)
        sqb = [Buf() for _ in range(8)]
        sa = [kb.sb("sa%d" % i, [128, N], F32) for i in range(2)]
        T = {n: kb.sb("T" + n, [128, N], F32) for n in ("mean", "tmp", "sd", "rstd")}
        psa = [kb.ps("psa%d" % i, [128, 512]) for i in range(2)]
        psu = [kb.ps("psu%d" % i, [128, 512]) for i in range(2)]
        psy = [kb.ps("psy%d" % i, [128, 512]) for i in range(2)]
        pst = [kb.ps("pst%d" % i, [128, 512]) for i in range(2)]

        def load(i):
            t0, col = blocks[i]
            kb.dma("sync", hbc[i % 3], hb[i % 3].t[:], srcv[:, :, t0:t0 + N])

        def A(i):
            t0, col = blocks[i]
            h, hc = hb[i % 3], hbc[i % 3]
            x, xb = xm[i % 2], xmb[i % 2]
            if i + 1 < len(blocks):
                load(i + 1)
            for k in range(8):
                kb.op("scalar", "activation", x.t[:, k, :], h.t[:, k, :], AF.Identity,
                      bias=modsT.t[:, l, m_sh * 8 + k, col:col + 1], scale=modsT.t[:, l, m_sc * 8 + k, col:col + 1],
                      rd=[hc[k], modsT], wr=[xb[k]])
            for jj in range(22):
                pa, pu = psa[jj % 2], psu[jj % 2]
                for k in range(8):
                    kb.op("tensor", "matmul", pa.t[:, 0:N], win[k].t[:, jj * 128:(jj + 1) * 128], x.t[:, k, :],
                          start=(k == 0), stop=(k == 7), rd=[win[k], xb[k]], wr=[pa])
                for k in range(8):
                    kb.op("tensor", "matmul", pu.t[:, 0:N], win[k].t[:, DFF + jj * 128:DFF + (jj + 1) * 128], x.t[:, k, :],
                          start=(k == 0), stop=(k == 7), rd=[win[k], xb[k]], wr=[pu])
                s = sa[jj % 2]
                kb.op("scalar", "activation", s.t[:], pa.t[:, 0:N], AF.Silu, rd=[pa], wr=[s])
                kb.op("vector", "tensor_tensor", g.t[:, jj, :], pu.t[:, 0:N], s.t[:], ALU.mult, rd=[pu, s], wr=[gb[jj]])

        def B(i):
            t0, col = blocks[i]
            h, hc = hb[i % 3], hbc[i % 3]
            for m in range(8):
                py = psy[m % 2]
                for q in range(22):
                    kb.op("tensor", "matmul", py.t[:, 0:N], wout[q].t[:, m * 128:(m + 1) * 128], g.t[:, q, :],
                          start=(q == 0), stop=(q == 21), rd=[wout[q], gb[q]], wr=[py])
                kb.op("vector", "scalar_tensor_tensor", h.t[:, m, :], py.t[:, 0:N],
                      modsT.t[:, l, m_g * 8 + m, col:col + 1], h.t[:, m, :], ALU.mult, ALU.add,
                      rd=[py, hc[m], modsT], wr=[hc[m]])
                kb.op("scalar", "activation", sq.t[:, m, :], h.t[:, m, :], AF.Square, rd=[hc[m]], wr=[sqb[m]])

        def C(i):
            t0, col = blocks[i]
            h, hc = hb[i % 3], hbc[i % 3]
            ln_tail(kb, G, h, hc, sq, sqb, T, N, l, lnj, pst)
            kb.dma("gpsimd", None, dstv[:, :, t0 + dst_off:t0 + dst_off + N], h.t[:], rd=hc, out=final)

        load(0)
        for i in range(len(blocks)):
            A(i)
            if i > 0:
                C(i - 1)
            B(i)
        C(len(blocks) - 1)


def host_consts():
    t = np.arange(NLAT)
    row = (t // 64).astype(np.float32)
    colp = (t % 64).astype(np.float32)
    inv = (np.float32(10000.0) ** (-np.arange(32, dtype=np.float32) / np.float32(32))).astype(np.float32)
    rope = np.zeros((128, 2, NLAT), np.float32)
    for d in range(128):
        a, bb, p = d // 64, (d // 32) % 2, d % 32
        ang = ((row if a == 0 else colp) * inv[p]).astype(np.float32)
        rope[d, 0] = np.cos(ang)
        rope[d, 1] = np.sin(ang) * (-1.0 if bb == 0 else 1.0)
    cm = np.zeros((128, 8, 128), np.float32)
    idx = np.arange(128)
    cm[idx ^ 32, 0, idx] = 1.0
    cm[:, 1, :] = 1.0 / 128.0
    triu = (idx[:, None] <= idx[None, :]).astype(np.float32)
    tril = (idx[:, None] >= idx[None, :]).astype(np.float32)
    cm[:, 2, :] = triu * (128.0 ** -0.5)
    cm[:, 3, :] = tril * (128.0 ** -0.5)
    cm[:, 4, :] = triu
    cm[:, 5, :] = tril
    cm[:, 6, :] = 1.0
    cm[:, 7, :] = np.eye(128, dtype=np.float32)
    return {"rope": rope, "cmat": cm}


def attn_stage(kb, io, G, src, dst, bs=range(NB)):
    N = NBLK
    modsT, vecs = G["modsT"], G["vecs"]
    srcv, dstv = fm(src), fm(dst)
    SC = 128.0 ** -0.5
    with kb.scope():
        wi = [kb.sb("awi%d" % k, [128, 1536], BF16) for k in range(8)]
        wo = [kb.sb("awo%d" % k, [128, D], BF16) for k in range(8)]
        for k in range(8):
            kb.dma("gpsimd", wi[k], wi[k].t[:], io["att_w_in"][0][k * 128:(k + 1) * 128, :])
        for k in range(8):
            kb.dma("gpsimd", wo[k], wo[k].t[:], io["att_w_out"][0][k * 128:(k + 1) * 128, :])
        rope = kb.sb("rope", [128, 2, NLAT], F32)
        kb.dma("sync", rope, rope.t[:], io["rope"][:, :, :])
        cm = kb.sb("cm", [128, 8, 128], F32)
        kb.dma("sync", cm, cm.t[:], io["cmat"][:, :, :])
        ones_bf = kb.sb("ones_bf", [128, 128], BF16)
        kb.op("gpsimd", "memset", ones_bf.t[:], 1.0, wr=[ones_bf])
        qT = kb.sb("qT", [128, 8, 2304], BF16)
        qTb = [[Buf() for _ in range(9)] for _ in range(8)]
        kT = kb.sb("kT", [128, 2, 2304], BF16)
        kTb = [[Buf() for _ in range(9)] for _ in range(2)]
        vt = kb.sb("vt", [128, 18, 256], BF16)
        vtb = [Buf() for _ in range(18)]
        hz = [kb.sb("hz%d" % i, [128, 8, N], F32) for i in range(2)]
        hzc = [[Buf() for _ in range(8)] for _ in range(2)]
        xm = [kb.sb("axm%d" % i, [128, 8, N], BF16) for i in range(2)]
        xmb = [[Buf() for _ in range(8)] for _ in range(2)]
        W = {n: [kb.sb("aw_%s%d" % (n, i), [128, N], F32) for i in range(2)] for n in ("sqh", "rs", "qn", "t1", "t2")}
        sq = kb.sb("asq", [128, 8, N], F32)
        sqb = [Buf() for _ in range(8)]
        attnT = [kb.sb("attnT%d" % i, [128, 8, N], BF16) for i in range(2)]
        atb = [[Buf() for _ in range(8)] for _ in range(2)]
        pT = [kb.sb("pT%d" % i, [128, N], BF16) for i in range(3)]
        rden = [kb.sb("rden%d" % i, [128, N], F32) for i in range(2)]
        T = {n: kb.sb("aT" + n, [128, N], F32) for n in ("mean", "tmp", "sd", "rstd")}
        P = [kb.ps("aP%d" % i, [128, 512]) for i in range(8)]
        st = {"hz": 0}

        def blk_info(b, blk):
            is_ctx = blk == 8
            t0 = ctx0(b) if is_ctx else lat0(b) + blk * N
            lc = 2048 if is_ctx else blk * N
            col = 4 if is_ctx else b
            return is_ctx, t0, lc, col

        def load_h(b, blk):
            i = st["hz"] % 2
            st["hz"] += 1
            _, t0, _, _ = blk_info(b, blk)
            kb.dma("sync", hzc[i], hz[i].t[:], srcv[:, :, t0:t0 + N])
            return i

        def proj_block(b, blk, hi, nxt):
            is_ctx, t0, lc, col = blk_info(b, blk)
            h, hc = hz[hi], hzc[hi]
            x, xb = xm[hi], xmb[hi]
            nhi = load_h(*nxt) if nxt is not None else None
            for k in range(8):
                kb.op("scalar", "activation", x.t[:, k, :], h.t[:, k, :], AF.Identity,
                      bias=modsT.t[:, 0, 3 * 8 + k, col:col + 1], scale=modsT.t[:, 0, 4 * 8 + k, col:col + 1],
                      rd=[hc[k], modsT], wr=[xb[k]])
            for hs in range(10):
                c0 = hs * 128 if hs < 8 else 1024 + (hs - 8) * 128
                i2 = hs % 2
                pq = P[i2]
                for k in range(8):
                    kb.op("tensor", "matmul", pq.t[:, 0:N], wi[k].t[:, c0:c0 + 128], x.t[:, k, :],
                          start=(k == 0), stop=(k == 7), rd=[wi[k], xb[k]], wr=[pq])
                sqh, rs, qn, t1, t2 = (W[n][i2] for n in ("sqh", "rs", "qn", "t1", "t2"))
                kb.op("scalar", "activation", sqh.t[:], pq.t[:, 0:N], AF.Square, rd=[pq], wr=[sqh])
                pss = P[2 + i2]
                kb.op("tensor", "matmul", pss.t[:, 0:N], cm.t[:, 1, :], sqh.t[:], start=True, stop=True,
                      rd=[cm, sqh], wr=[pss])
                kb.op("vector", "tensor_scalar", rs.t[:], pss.t[:, 0:N], 1e-6, None, ALU.add, rd=[pss], wr=[rs])
                kb.op("scalar", "activation", rs.t[:], rs.t[:], AF.Sqrt, rd=[rs], wr=[rs])
                kb.op("vector", "reciprocal", rs.t[:], rs.t[:], rd=[rs], wr=[rs])
                gcol = VC_QG if hs < 8 else VC_KG
                kb.op("vector", "scalar_tensor_tensor", qn.t[:], pq.t[:, 0:N], vecs.t[:, gcol:gcol + 1], rs.t[:],
                      ALU.mult, ALU.mult, rd=[pq, vecs, rs], wr=[qn])
                if hs < 8:
                    dest, dbuf = qT.t[:, hs, lc:lc + N], qTb[hs][blk]
                else:
                    dest, dbuf = kT.t[:, hs - 8, lc:lc + N], kTb[hs - 8][blk]
                if not is_ctx:
                    pp = P[4 + i2]
                    kb.op("tensor", "matmul", pp.t[:, 0:N], cm.t[:, 0, :], qn.t[:], start=True, stop=True,
                          rd=[cm, qn], wr=[pp])
                    pos = blk * N
                    kb.op("gpsimd", "tensor_tensor", t1.t[:], qn.t[:], rope.t[:, 0, pos:pos + N], ALU.mult,
                          rd=[qn, rope], wr=[t1])
                    kb.op("vector", "tensor_tensor", t2.t[:], pp.t[:, 0:N], rope.t[:, 1, pos:pos + N], ALU.mult,
                          rd=[pp, rope], wr=[t2])
                    kb.op("gpsimd", "tensor_tensor", dest, t1.t[:], t2.t[:], ALU.add, rd=[t1, t2], wr=[dbuf])
                else:
                    kb.op("gpsimd", "tensor_copy", dest, qn.t[:], rd=[qn], wr=[dbuf])
            for tl in range(2):
                pv = P[6 + tl]
                for k in range(8):
                    kb.op("tensor", "matmul", pv.t[:, 0:256], x.t[:, k, tl * 128:(tl + 1) * 128], wi[k].t[:, 1280:1536],
                          start=(k == 0), stop=(k == 7), rd=[xb[k], wi[k]], wr=[pv])
                kt = lc // 128 + tl
                kb.op("scalar", "activation", vt.t[:, kt, :], pv.t[:, 0:256], AF.Identity, rd=[pv], wr=[vtb[kt]])
            return nhi

        def attention(b):
            units = []
            for qb in range(9):
                kts = list(range(18)) if qb < 8 else [16, 17]
                for h in range(8):
                    for j, kt in enumerate(kts):
                        units.append((qb, h, kt, j == 0, j == len(kts) - 1))
            hz_of = {}

            def S(u, i):
                qb, h, kt, first, last = u
                lc = qb * N
                ps = P[i % 2]
                kb.op("tensor", "matmul", ps.t[:, 0:N], kT.t[:, h // 4, kt * 128:(kt + 1) * 128], qT.t[:, h, lc:lc + N],
                      start=True, stop=True, rd=[kTb[h // 4][kt // 2], qTb[h][qb]], wr=[ps])

            def tail1(qb):
                _, t0, lc, col = blk_info(b, qb)
                hi = hz_of[qb]
                h, hc = hz[hi], hzc[hi]
                at, ab = attnT[qb % 2], atb[qb % 2]
                for m in range(8):
                    py = P[6 + m % 2]
                    for hh in range(8):
                        kb.op("tensor", "matmul", py.t[:, 0:N], wo[hh].t[:, m * 128:(m + 1) * 128], at.t[:, hh, :],
                              start=(hh == 0), stop=(hh == 7), rd=[wo[hh], ab[hh]], wr=[py])
                    kb.op("vector", "scalar_tensor_tensor", h.t[:, m, :], py.t[:, 0:N],
                          modsT.t[:, 0, 5 * 8 + m, col:col + 1], h.t[:, m, :], ALU.mult, ALU.add,
                          rd=[py, hc[m], modsT], wr=[hc[m]])
                    kb.op("scalar", "activation", sq.t[:, m, :], h.t[:, m, :], AF.Square, rd=[hc[m]], wr=[sqb[m]])

            def tail2(qb):
                _, t0, lc, col = blk_info(b, qb)
                hi = hz_of[qb]
                ln_tail(kb, G, hz[hi], hzc[hi], sq, sqb, T, N, 0, 1, (P[6], P[7]))
                kb.dma("gpsimd", None, dstv[:, :, t0:t0 + N], hz[hi].t[:], rd=hzc[hi])

            pending = []
            hcount = 0
            for i, u in enumerate(units):
                qb, h, kt, first, last = u
                if i == 0:
                    S(u, 0)
                if first and h == 0:
                    hz_of[qb] = load_h(b, qb)
                if i + 1 < len(units):
                    S(units[i + 1], i + 1)
                ps, p = P[i % 2], pT[i % 3]
                kb.op("scalar", "activation", p.t[:], ps.t[:, 0:N], AF.Exp, scale=SC, rd=[ps], wr=[p])
                po, pd = P[2 + hcount % 2], P[4 + hcount % 2]
                kv = h // 4
                kb.op("tensor", "matmul", po.t[:, 0:N], vt.t[:, kt, kv * 128:(kv + 1) * 128], p.t[:],
                      start=first, stop=last, rd=[vtb[kt], p], wr=[po])
                kb.op("tensor", "matmul", pd.t[:, 0:N], ones_bf.t[:], p.t[:], start=first, stop=last,
                      rd=[ones_bf, p], wr=[pd])
                if last:
                    rd_ = rden[hcount % 2]
                    kb.op("vector", "reciprocal", rd_.t[:], pd.t[:, 0:N], rd=[pd], wr=[rd_])
                    kb.op("vector", "tensor_tensor", attnT[qb % 2].t[:, h, :], po.t[:, 0:N], rd_.t[:], ALU.mult,
                          rd=[po, rd_], wr=[atb[qb % 2][h]])
                    hcount += 1
                    if h == 7:
                        pending.append((i + 4, tail1, qb))
                        pending.append((i + 10, tail2, qb))
                while pending and pending[0][0] <= i:
                    _, fn, a = pending.pop(0)
                    fn(a)
            for _, fn, a in pending:
                fn(a)

        for b in bs:
            order = list(range(9))
            hi = load_h(b, 0)
            for j, blk in enumerate(order):
                nxt = (b, order[j + 1]) if j + 1 < len(order) else None
                hi = proj_block(b, blk, hi, nxt)
            attention(b)


def mlstm_stage(kb, io, G, src, dst, bs=range(NB)):
    N = NBLK
    L1 = 1
    modsT, vecs = G["modsT"], G["vecs"]
    srcv, dstv = fm(src), fm(dst)
    SC = 128.0 ** -0.5
    wml = io["ml_w_in"][0]
    HF, SIG = io["HF"], io["SIG"]
    with kb.scope():
        cm = kb.sb("mcm", [128, 8, 128], F32)
        kb.dma("sync", cm, cm.t[:], io["cmat"][:, :, :])
        ident = kb.sb("ident", [128, 128], BF16)
        kb.op("vector", "tensor_copy", ident.t[:], cm.t[:, 7, :], rd=[cm], wr=[ident])
        ngb = kb.sb("ngb", [128, D], F32)
        kb.dma("sync", ngb, ngb.t[:], io["ngb"][:, :])
        gbb = kb.sb("gbb", [128, 16], F32)
        kb.dma("sync", gbb, gbb.t[:], io["gbb"][:, :])
        qT = kb.sb("mqT", [128, 4, 2304], BF16)
        qTb = [[Buf() for _ in range(9)] for _ in range(4)]
        kT = kb.sb("mkT", [128, 4, 2304], BF16)
        kTb = [[Buf() for _ in range(9)] for _ in range(4)]
        ktok = kb.sb("ktok", [128, 18, 4, 128], BF16)
        ktb = [Buf() for _ in range(18)]
        vext = kb.sb("vext", [128, 18, 4, 258], BF16)
        vxb = [Buf() for _ in range(18)]
        vones = Buf()
        kb.op("gpsimd", "memset", vext.t[:, :, :, 256:257], 1.0, wr=[vones])
        kb.op("gpsimd", "memset", vext.t[:, :, :, 257:258], 0.0, wr=[vones])
        aa = kb.sb("aa", [128, 18, 8], F32)
        bqa = kb.sb("bqa", [128, 18, 8], F32)
        edec = kb.sb("edec", [128, 18, 8], F32)
        nbqa = kb.sb("nbqa", [128, 18, 8], F32)
        scb = [Buf() for _ in range(18)]
        P = [kb.ps("mP%d" % i, [128, 512]) for i in range(6)]
        Pb = kb.ps("mPb", [128, 8, 128], BF16)
        P7 = kb.ps("mP7", [128, 512])

        def blk_info(b, blk):
            t0 = ctx0(b) if blk == 0 else lat0(b) + (blk - 1) * N
            col = 4 if blk == 0 else b
            return t0, blk * N, col

        def phase1(b):
            with kb.scope():
                wqk = [kb.sb("wqk%d" % k, [128, 1024], BF16) for k in range(8)]
                wv = [kb.sb("wv%d" % k, [128, 1024], BF16) for k in range(8)]
                wo_ = [kb.sb("wog%d" % k, [128, 1024], BF16) for k in range(8)]
                wg = [kb.sb("wg%d" % k, [128, 16], BF16) for k in range(8)]
                for k in range(8):
                    rows = slice(k * 128, (k + 1) * 128)
                    kb.dma("gpsimd", wqk[k], wqk[k].t[:], wml[rows, 0:1024])
                    kb.dma("gpsimd", wv[k], wv[k].t[:], wml[rows, 1024:2048])
                    kb.dma("gpsimd", wo_[k], wo_[k].t[:], wml[rows, 2048:3072])
                    kb.dma("gpsimd", wg[k], wg[k].t[:], wml[rows, 3072:3088])
                xh = [kb.sb("xh%d" % i, [128, 8, 258], F32) for i in range(2)]
                xmm = [kb.sb("mxm%d" % i, [128, 8, 258], BF16) for i in range(2)]
                acc = [kb.sb("acc%d" % i, [128, N], F32) for i in range(2)]
                gs = kb.sb("gs", [128, 16], F32)
                lf = kb.sb("lf", [128, 8], F32)
                tmpa = kb.sb("tmpa", [128, 8], F32)
                sigt = [kb.sb("sigt%d" % i, [128, D], F32) for i in range(2)]
                nsig = 0

                def load_x(blk):
                    t0, lc, col = blk_info(b, blk)
                    x = xh[blk % 2]
                    hasl = blk >= 2
                    hasr = 1 <= blk <= 7
                    kb.op("gpsimd", "memset", x.t[:, :, 0:1], 0.0, wr=[x])
                    kb.op("gpsimd", "memset", x.t[:, :, 257:258], 0.0, wr=[x])
                    lo = 0 if hasl else 1
                    hi = 258 if hasr else 257
                    kb.dma("sync", x, x.t[:, :, lo:hi], srcv[:, :, t0 - 1 + lo:t0 - 1 + hi])

                load_x(0)
                for blk in range(9):
                    t0, lc, col = blk_info(b, blk)
                    if blk + 1 < 9:
                        load_x(blk + 1)
                    x, xm_ = xh[blk % 2], xmm[blk % 2]
                    for k in range(8):
                        kb.op("scalar", "activation", xm_.t[:, k, :], x.t[:, k, :], AF.Identity,
                              bias=modsT.t[:, L1, 3 * 8 + k, col:col + 1], scale=modsT.t[:, L1, 4 * 8 + k, col:col + 1],
                              rd=[x, modsT], wr=[xm_])
                    if not blk >= 2:
                        kb.op("gpsimd", "memset", xm_.t[:, :, 0:1], 0.0, wr=[xm_])
                    if not 1 <= blk <= 7:
                        kb.op("gpsimd", "memset", xm_.t[:, :, 257:258], 0.0, wr=[xm_])
                    for c8 in range(8):
                        pqk = P[c8 % 2]
                        for k in range(8):
                            kb.op("tensor", "matmul", pqk.t[:, 0:258], wqk[k].t[:, c8 * 128:(c8 + 1) * 128], xm_.t[:, k, :],
                                  start=(k == 0), stop=(k == 7), rd=[wqk[k], xm_], wr=[pqk])
                        a_ = acc[c8 % 2]
                        w0, w1, w2, cb = (VC_CW + c8, VC_CW + 8 + c8, VC_CW + 16 + c8, VC_CB + c8)
                        kb.op("vector", "tensor_scalar", a_.t[:], pqk.t[:, 0:256], vecs.t[:, w0:w0 + 1], vecs.t[:, cb:cb + 1],
                              ALU.mult, ALU.add, rd=[pqk, vecs], wr=[a_])
                        kb.op("vector", "scalar_tensor_tensor", a_.t[:], pqk.t[:, 1:257], vecs.t[:, w1:w1 + 1], a_.t[:],
                              ALU.mult, ALU.add, rd=[pqk, vecs, a_], wr=[a_])
                        kb.op("vector", "scalar_tensor_tensor", a_.t[:], pqk.t[:, 2:258], vecs.t[:, w2:w2 + 1], a_.t[:],
                              ALU.mult, ALU.add, rd=[pqk, vecs, a_], wr=[a_])
                        if c8 < 4:
                            dest, dbuf = qT.t[:, c8, lc:lc + N], qTb[c8][blk]
                        else:
                            dest, dbuf = kT.t[:, c8 - 4, lc:lc + N], kTb[c8 - 4][blk]
                        kb.op("scalar", "activation", dest, a_.t[:], AF.Silu, rd=[a_], wr=[dbuf])
                    for tl in range(2):
                        c = 2 * blk + tl
                        xs = slice(1 + tl * 128, 1 + (tl + 1) * 128)
                        for h in range(4):
                            kb.op("tensor", "transpose", Pb.t[:, h, :], kT.t[:, h, lc + tl * 128:lc + (tl + 1) * 128], ident.t[:],
                                  rd=[kTb[h][blk], ident], wr=[Pb])
                        kb.op("scalar", "activation", ktok.t[:, c, :, :], Pb.t[:, 0:4, :], AF.Identity, rd=[Pb], wr=[ktb[c]])
                        pg = P[4]
                        for k in range(8):
                            kb.op("tensor", "matmul", pg.t[:, 0:16], xm_.t[:, k, xs], wg[k].t[:, 0:16],
                                  start=(k == 0), stop=(k == 7), rd=[xm_, wg[k]], wr=[pg])
                        kb.op("vector", "tensor_tensor", gs.t[:], pg.t[:, 0:16], gbb.t[:], ALU.add, rd=[pg, gbb], wr=[gs])
                        gsv = gs.t[:].rearrange("p (d t h) -> p d t h", d=2, t=2)
                        lfv = lf.t[:].rearrange("p (d h) -> p d h", d=2)
                        kb.op("scalar", "activation", lfv, gsv[:, :, 1, :], AF.Exp, scale=-1.0, rd=[gs], wr=[lf])
                        kb.op("vector", "tensor_scalar", lf.t[:], lf.t[:], 1.0, None, ALU.add, rd=[lf], wr=[lf])
                        kb.op("scalar", "activation", lf.t[:], lf.t[:], AF.Ln, rd=[lf], wr=[lf])
                        pc = P[5]
                        kb.op("tensor", "matmul", pc.t[:, 0:4], cm.t[:, 4, :], lf.t[:, 0:4], start=True, stop=True,
                              rd=[cm, lf], wr=[pc])
                        kb.op("tensor", "matmul", pc.t[:, 4:8], cm.t[:, 5, :], lf.t[:, 4:8], start=True, stop=True,
                              rd=[cm, lf], wr=[pc])
                        kb.op("tensor", "matmul", pc.t[:, 8:16], cm.t[:, 6, :], lf.t[:, 0:8], start=True, stop=True,
                              rd=[cm, lf], wr=[pc])
                        tav = tmpa.t[:].rearrange("p (d h) -> p d h", d=2)
                        pcv = pc.t[:, 0:8].rearrange("p (d h) -> p d h", d=2)
                        kb.op("vector", "tensor_tensor", tav, pcv, gsv[:, :, 0, :], ALU.add, rd=[pc, gs], wr=[tmpa])
                        kb.op("scalar", "activation", bqa.t[:, c, :], pc.t[:, 0:8], AF.Exp, scale=-1.0, rd=[pc, tmpa], wr=[scb[c]])
                        kb.op("scalar", "activation", edec.t[:, c, :], pc.t[:, 8:16], AF.Exp, scale=-1.0, rd=[pc], wr=[scb[c]])
                        kb.op("vector", "tensor_scalar", nbqa.t[:, c, :], bqa.t[:, c, :], -1.0, None, ALU.mult, rd=[scb[c]], wr=[scb[c]])
                        kb.op("scalar", "activation", aa.t[:, c, :], tmpa.t[:], AF.Exp, rd=[tmpa], wr=[scb[c]])
                        for half in range(2):
                            pv = P[2 + half]
                            for k in range(8):
                                kb.op("tensor", "matmul", pv.t[:, 0:512], xm_.t[:, k, xs], wv[k].t[:, half * 512:(half + 1) * 512],
                                      start=(k == 0), stop=(k == 7), rd=[xm_, wv[k]], wr=[pv])
                            kb.op("scalar", "activation", vext.t[:, c, 2 * half:2 * half + 2, 0:256],
                                  pv.t[:, 0:512].rearrange("p (h v) -> p h v", h=2), AF.Identity, rd=[pv], wr=[vxb[c]])
                        if blk >= 1:
                            sg = sigt[nsig % 2]
                            nsig += 1
                            for half in range(2):
                                po = P[2 + half]
                                for k in range(8):
                                    kb.op("tensor", "matmul", po.t[:, 0:512], xm_.t[:, k, xs], wo_[k].t[:, half * 512:(half + 1) * 512],
                                          start=(k == 0), stop=(k == 7), rd=[xm_, wo_[k]], wr=[po])
                                kb.op("scalar", "activation", sg.t[:, half * 512:(half + 1) * 512], po.t[:, 0:512], AF.Sigmoid,
                                      rd=[po], wr=[sg])
                            kb.dma("gpsimd", G["sigd"][b][c - 2], SIG[b, c - 2], sg.t[:], rd=[sg])

        def scan(b, direction):
            fwd = direction == 0
            order = list(range(18)) if fwd else [1, 0] + list(range(17, 1, -1))
            dcol = 0 if fwd else 4
            mask = cm.t[:, 2, :] if fwd else cm.t[:, 3, :]
            with kb.scope():
                cext = kb.sb("cext", [128, 4, 258], F32)
                cb_ = kb.sb("cb", [128, 4, 258], BF16)
                cxb = [Buf() for _ in range(4)]
                cbb = [Buf() for _ in range(4)]
                kb.op("gpsimd", "memset", cext.t[:], 0.0, wr=cxb)
                kb.op("gpsimd", "memset", cb_.t[:], 0.0, wr=cbb)
                sm = [kb.sb("sm%d" % i, [128, 128], BF16) for i in range(2)]
                ka = [kb.sb("ka%d" % i, [128, 128], BF16) for i in range(2)]
                tmpc = [kb.sb("tmpc%d" % i, [128, 258], F32) for i in range(2)]
                dd = [kb.sb("dd%d" % i, [128, 4], F32) for i in range(2)]
                hsum = [kb.sb("hsum%d" % i, [128, D], F32) for i in range(2)]
                if not fwd:
                    wout = [kb.sb("mwo%d" % k, [128, D], BF16) for k in range(8)]
                    for k in range(8):
                        kb.dma("gpsimd", wout[k], wout[k].t[:], io["ml_w_out"][0][k * 128:(k + 1) * 128, :])
                    hfl = [kb.sb("hfl%d" % i, [128, D], F32) for i in range(2)]
                    sgl = [kb.sb("sgl%d" % i, [128, D], F32) for i in range(2)]
                    rbf = kb.sb("rbf", [128, D], BF16)
                    rT = kb.sb("rT", [128, 8, N], BF16)
                    rTb = [Buf() for _ in range(2)]
                    h4 = [kb.sb("h4%d" % i, [128, 8, N], F32) for i in range(2)]
                    h4c = [[Buf() for _ in range(8)] for _ in range(2)]
                    sq = kb.sb("msq", [128, 8, N], F32)
                    sqb = [Buf() for _ in range(8)]
                    T = {n: kb.sb("mT" + n, [128, N], F32) for n in ("mean", "tmp", "sd", "rstd")}
                    st6 = kb.sb("st6", [128, 4, 6], F32)
                    mv = kb.sb("mv", [128, 4, 2], F32)
                    rs4 = kb.sb("rs4", [128, 4], F32)

                units = [(c, h) for c in order for h in range(4)]
                latu = [u for u in units if u[0] >= 2]

                def S(u, j):
                    c, h = u
                    ps = P[j % 2]
                    cs = slice(c * 128, (c + 1) * 128)
                    kb.op("tensor", "matmul", ps.t[:, 0:128], kT.t[:, h, cs], qT.t[:, h, cs], start=True, stop=True,
                          rd=[kTb[h][c // 2], qTb[h][c // 2]], wr=[ps])

                def prefetch(c):
                    i = (c // 1) % 2
                    kb.dma("sync", hfl[i], hfl[i].t[:], HF[b, c - 2], rd=[G["hfd"][b][c - 2]])
                    kb.dma("sync", sgl[i], sgl[i].t[:], SIG[b, c - 2], rd=[G["sigd"][b][c - 2]])

                def load_h4(bi):
                    i = bi % 2
                    t0 = lat0(b) + bi * N
                    kb.dma("sync", h4c[i], h4[i].t[:], srcv[:, :, t0:t0 + N])

                def readout(c):
                    i = c % 2
                    hs = hsum[i]
                    for h in range(4):
                        kb.op("vector", "bn_stats", st6.t[:, h, :], hs.t[:, h * 256:(h + 1) * 256], rd=[hs], wr=[st6])
                    for h in range(4):
                        kb.op("vector", "bn_aggr", mv.t[:, h, :], st6.t[:, h, :], rd=[st6], wr=[mv])
                    kb.op("vector", "tensor_scalar", rs4.t[:], mv.t[:, :, 1], 1e-5, None, ALU.add, rd=[mv], wr=[rs4])
                    kb.op("scalar", "activation", rs4.t[:], rs4.t[:], AF.Sqrt, rd=[rs4], wr=[rs4])
                    kb.op("vector", "reciprocal", rs4.t[:], rs4.t[:], rd=[rs4], wr=[rs4])
                    for h in range(4):
                        kb.op("vector", "tensor_scalar", hs.t[:, h * 256:(h + 1) * 256], hs.t[:, h * 256:(h + 1) * 256],
                              mv.t[:, h, 0:1], rs4.t[:, h:h + 1], ALU.subtract, ALU.mult, rd=[hs, mv, rs4], wr=[hs])
                    kb.op("gpsimd", "tensor_tensor", hs.t[:], hs.t[:], ngb.t[:], ALU.mult, rd=[hs, ngb], wr=[hs])
                    kb.op("gpsimd", "tensor_tensor", rbf.t[:], hs.t[:], sgl[i].t[:], ALU.mult, rd=[hs, sgl[i]], wr=[rbf])
                    for k in range(8):
                        kb.op("tensor", "transpose", Pb.t[:, k, :], rbf.t[:, k * 128:(k + 1) * 128], ident.t[:],
                              rd=[rbf, ident], wr=[Pb])
                    half = (c - 2) % 2
                    kb.op("scalar", "activation", rT.t[:, :, half * 128:(half + 1) * 128], Pb.t[:, :, :], AF.Identity,
                          rd=[Pb], wr=[rTb[half]])

                def tail(bi):
                    i = bi % 2
                    h, hc = h4[i], h4c[i]
                    t0 = lat0(b) + bi * N
                    for m in range(8):
                        py = P7
                        for k in range(8):
                            kb.op("tensor", "matmul", py.t[:, 0:N], wout[k].t[:, m * 128:(m + 1) * 128], rT.t[:, k, :],
                                  start=(k == 0), stop=(k == 7), rd=[wout[k]] + rTb, wr=[py])
                        kb.op("vector", "scalar_tensor_tensor", h.t[:, m, :], py.t[:, 0:N],
                              modsT.t[:, L1, 5 * 8 + m, b:b + 1], h.t[:, m, :], ALU.mult, ALU.add,
                              rd=[py, hc[m], modsT], wr=[hc[m]])
                        kb.op("scalar", "activation", sq.t[:, m, :], h.t[:, m, :], AF.Square, rd=[hc[m]], wr=[sqb[m]])
                    ln_tail(kb, G, h, hc, sq, sqb, T, N, L1, 1, ((P7, P7.t[:, 0:N]), (P7, P7.t[:, N:2 * N])))
                    kb.dma("gpsimd", None, dstv[:, :, t0:t0 + N], h.t[:], rd=hc)

                if not fwd:
                    prefetch(17)
                    load_h4(7)
                jl = 0
                if latu:
                    pass
                first_lat_emitted = False
                for n, (c, h) in enumerate(units):
                    is_lat = c >= 2
                    last_chunk = (c == order[-1])
                    a_ap = aa.t[:, c, dcol + h:dcol + h + 1]
                    bq_ap = bqa.t[:, c, dcol + h:dcol + h + 1]
                    e_ap = edec.t[:, c, dcol + h:dcol + h + 1]
                    nbq_ap = nbqa.t[:, c, dcol + h:dcol + h + 1]
                    if not last_chunk:
                        k_ = ka[n % 2]
                        kb.op("gpsimd", "tensor_scalar", k_.t[:], ktok.t[:, c, h, :], a_ap, None, ALU.mult,
                              rd=[ktb[c], scb[c]], wr=[k_])
                        pu = P[4 + n % 2]
                        kb.op("tensor", "matmul", pu.t[:, 0:257], k_.t[:], vext.t[:, c, h, 0:257], start=True, stop=True,
                              rd=[k_, vxb[c], vones], wr=[pu])
                    if is_lat:
                        if not first_lat_emitted:
                            S((c, h), jl)
                            first_lat_emitted = True
                        if jl + 1 < len(latu):
                            S(latu[jl + 1], jl + 1)
                        ps = P[jl % 2]
                        s_ = sm[jl % 2]
                        kb.op("vector", "scalar_tensor_tensor", s_.t[:], ps.t[:, 0:128], a_ap, mask, ALU.mult, ALU.mult,
                              rd=[ps, scb[c], cm], wr=[s_])
                        pout = P[2 + jl % 2]
                        cs = slice(c * 128, (c + 1) * 128)
                        kb.op("tensor", "matmul", pout.t[:, 0:257], s_.t[:], vext.t[:, c, h, 0:257], start=True, stop=False,
                              rd=[s_, vxb[c], vones], wr=[pout])
                        kb.op("tensor", "matmul", pout.t[:, 0:257], qT.t[:, h, cs], cb_.t[:, h, 0:257], start=False, stop=True,
                              rd=[qTb[h][c // 2], cbb[h]], wr=[pout])
                    if not last_chunk:
                        tc_ = tmpc[n % 2]
                        kb.op("vector", "scalar_tensor_tensor", tc_.t[:, 0:257], pu.t[:, 0:257], SC, cext.t[:, h, 0:257],
                              ALU.mult, ALU.add, rd=[pu, cxb[h]], wr=[tc_])
                        kb.op("vector", "tensor_scalar", cext.t[:, h, 0:257], tc_.t[:, 0:257], e_ap, None, ALU.mult,
                              rd=[tc_, scb[c]], wr=[cxb[h]])
                        kb.op("scalar", "activation", cb_.t[:, h, 0:257], cext.t[:, h, 0:257], AF.Identity,
                              rd=[cxb[h]], wr=[cbb[h]])
                    if is_lat:
                        d = dd[jl % 2]
                        kb.op("vector", "tensor_scalar", d.t[:, 0:1], pout.t[:, 256:257], bq_ap, 1.0, ALU.mult, ALU.max,
                              rd=[pout, scb[c]], wr=[d])
                        kb.op("vector", "tensor_scalar", d.t[:, 1:2], pout.t[:, 256:257], nbq_ap, 1.0, ALU.mult, ALU.max,
                              rd=[pout, scb[c]], wr=[d])
                        kb.op("vector", "tensor_tensor", d.t[:, 0:1], d.t[:, 0:1], d.t[:, 1:2], ALU.max, rd=[d], wr=[d])
                        kb.op("vector", "reciprocal", d.t[:, 1:2], d.t[:, 0:1], rd=[d], wr=[d])
                        kb.op("vector", "tensor_tensor", d.t[:, 2:3], d.t[:, 1:2], bq_ap, ALU.mult, rd=[d, scb[c]], wr=[d])
                        hs = hsum[c % 2]
                        if fwd:
                            kb.op("vector", "tensor_scalar", hs.t[:, h * 256:(h + 1) * 256], pout.t[:, 0:256], d.t[:, 2:3], None,
                                  ALU.mult, rd=[pout, d], wr=[hs])
                        else:
                            kb.op("vector", "scalar_tensor_tensor", hs.t[:, h * 256:(h + 1) * 256], pout.t[:, 0:256], d.t[:, 2:3],
                                  hfl[c % 2].t[:, h * 256:(h + 1) * 256], ALU.mult, ALU.add,
                                  rd=[pout, d, hfl[c % 2]], wr=[hs])
                        jl += 1
                    if is_lat and h == 3:
                        if fwd:
                            kb.dma("gpsimd", G["hfd"][b][c - 2], HF[b, c - 2], hsum[c % 2].t[:], rd=[hsum[c % 2]])
                        else:
                            if c - 1 >= 2:
                                prefetch(c - 1)
                            readout(c)
                            if (c - 2) % 2 == 0:
                                bi = (c - 2) // 2
                                tail(bi)
                                if bi - 1 >= 0:
                                    load_h4(bi - 1)

        for b in bs:
            phase1(b)
            scan(b, 0)
            scan(b, 1)

def host_vecs(inp):
    def ch(v):
        return np.ascontiguousarray(np.asarray(v, np.float32).reshape(-1, 128).T)
    cols = []
    for l in range(2):
        cols.append(ch(inp["ada_b"][l]))
    for l in range(2):
        for j in range(3):
            cols.append(ch(inp["ln_g"][l, j]))
    for l in range(2):
        for j in range(3):
            cols.append(ch(inp["ln_b"][l, j]))
    cols.append(ch(inp["att_q_gain"][0]))
    cols.append(ch(inp["att_k_gain"][0]))
    for tap in range(3):
        cols.append(ch(inp["ml_conv_w"][0, tap]))
    cols.append(ch(inp["ml_conv_b"][0]))
    v = np.concatenate(cols, axis=1)
    assert v.shape == (128, NVEC), v.shape
    return np.ascontiguousarray(v)


def host_bcast(inp):
    ngb = np.ascontiguousarray(np.broadcast_to(np.asarray(inp["ml_norm_g"][0], np.float32)[None, :], (128, D)))
    gbb = np.ascontiguousarray(np.broadcast_to(np.asarray(inp["ml_gate_b"][0], np.float32)[None, :], (128, 16)))
    return {"ngb": ngb, "gbb": gbb}


def host_core_inputs(inp, core):
    bs = slice(core * NB, (core + 1) * NB)
    x = np.asarray(inp["x"][bs], np.float32)
    ctx = np.asarray(inp["ctx"][bs], np.float32)
    hT0 = np.empty((D, TOK), np.float32)
    for b in range(NB):
        hT0[:, ctx0(b):ctx0(b) + NCTX] = ctx[b].T
        hT0[:, lat0(b):lat0(b) + NLAT] = x[b].T
    c = np.asarray(inp["c"][bs], np.float32)
    cc = np.concatenate([c, np.asarray(inp["c_ctx"], np.float32)[None, :], np.zeros((1, D), np.float32)], axis=0)
    cT = np.ascontiguousarray(cc.T.reshape(8, 128, 6).transpose(1, 0, 2))
    return {"hT0": hT0, "cT": cT}


WEIGHT_NAMES = ["ada_w", "ffn_w_in", "ffn_w_out", "att_w_in", "att_w_out", "ml_w_in", "ml_w_out"]


def declare_io(nc, debug_outs=(), need=None):
    io = {}
    shapes = {
        "hT0": [D, TOK], "cT": [128, 8, 6], "vecs": [128, NVEC],
        "ada_w": [2, D, 9 * D], "ffn_w_in": [2, 2, D, 2 * DFF], "ffn_w_out": [2, 2, DFF, D],
        "att_w_in": [1, D, 1536], "att_w_out": [1, D, D], "ml_w_in": [1, D, 3088], "ml_w_out": [1, D, D],
        "rope": [128, 2, NLAT], "cmat": [128, 8, 128], "ngb": [128, D], "gbb": [128, 16],
    }
    if need is not None:
        shapes = {n: s for n, s in shapes.items() if n in need}
    for n, s in shapes.items():
        io[n] = nc.dram_tensor(n, s, F32, kind="ExternalInput").ap()
    for n in ("H1", "H2", "H3", "H4", "H5"):
        kind = "ExternalOutput" if n in debug_outs else "Internal"
        io[n] = nc.dram_tensor(n, [D, TOK], F32, kind=kind).ap()
    io["OUT"] = nc.dram_tensor("OUT", [D, NB * NLAT], F32, kind="ExternalOutput").ap()
    io["HF"] = nc.dram_tensor("HF", [NB, 16, 128, D], F32).ap()
    io["SIG"] = nc.dram_tensor("SIG", [NB, 16, 128, D], F32).ap()
    return io


def alloc_globals(kb):
    G = {}
    G["modsT"] = kb.sb("modsT", [128, 2, 72, 6], F32)
    G["vecs"] = kb.sb("vecs", [128, NVEC], F32)
    G["ones_ln"] = kb.sb("ones_ln", [128, 128], F32)
    G["hfd"] = [[Buf() for _ in range(16)] for _ in range(NB)]
    G["sigd"] = [[Buf() for _ in range(16)] for _ in range(NB)]
    return G


def build_program(nbs=NB, debug_outs=()):
    nc = bass.Bass("TRN2", target_bir_lowering=False)
    io = declare_io(nc, debug_outs=debug_outs)
    blocks = all_blocks()
    lat_blocks = [bk for bk in blocks if bk[1] != 4]
    kb = KB(nc)
    with kb:
        G = alloc_globals(kb)
        prologue(kb, io, G)
        ffn_stage(kb, io, G, io["hT0"], io["H1"], 0, 0, blocks)
        attn_stage(kb, io, G, io["H1"], io["H2"])
        ffn_stage(kb, io, G, io["H2"], io["H3"], 0, 1, blocks)
        ffn_stage(kb, io, G, io["H3"], io["H4"], 1, 0, blocks)
        mlstm_stage(kb, io, G, io["H4"], io["H5"])
        ffn_stage(kb, io, G, io["H5"], io["OUT"], 1, 1, lat_blocks, dst_off=-NB * NCTX, final=True)
        kb.finish()
    return nc, kb


def host_inputs_for_core(inputs, core, shared):
    m = host_core_inputs(inputs, core)
    m.update(shared)
    return m


def host_shared(inputs):
    sh = {"vecs": host_vecs(inputs)}
    sh.update(host_consts())
    sh.update(host_bcast(inputs))
    for n in WEIGHT_NAMES:
        sh[n] = np.ascontiguousarray(np.asarray(inputs[n], np.float32))
    return sh


def gather_out(o):
    return np.ascontiguousarray(o.reshape(D, NB, NLAT).transpose(1, 2, 0))


def kernel(**inputs):
    nc, _ = build_program()
    shared = host_shared(inputs)
    in_maps = [host_inputs_for_core(inputs, c, shared) for c in range(NCORE)]
    res = run_bass_kernel_spmd(nc, in_maps, core_ids=list(range(NCORE)))
    out = np.empty((NCORE * NB, NLAT, D), np.float32)
    for c in range(NCORE):
        out[c * NB:(c + 1) * NB] = gather_out(np.asarray(res.results[c]["OUT"]))
    return out
```

```python
import numpy as np
from contextlib import ExitStack
import concourse.bass as bass
import concourse.mybir as mybir
from concourse.bass_utils import run_bass_kernel_spmd

F32 = mybir.dt.float32
BF16 = mybir.dt.bfloat16
AF = mybir.ActivationFunctionType
ALU = mybir.AluOpType


class Buf:
    __slots__ = ("t", "w", "r", "name")

    def __init__(self, t=None, name=""):
        self.t = t
        self.w = None
        self.r = {}
        self.name = name


class KB:
    ENGS = ("tensor", "vector", "scalar", "gpsimd", "sync")
    NDMA = 32
    EPOCH = 30000

    def __init__(self, nc):
        self.nc = nc
        self.stack = ExitStack()

    def __enter__(self):
        nc = self.nc
        self.stack.__enter__()
        self.semstack = self.stack
        self.old_tokens = []
        self.eng = {n: getattr(nc, n) for n in self.ENGS}
        self.nsem = 0
        self.esem = {}
        self.ecnt = {}
        self.waited = {n: {} for n in self.ENGS}
        for n in self.ENGS:
            self._new_esem(n)
        self.dsem = {"hw": [], "sw": []}
        for r in ("hw", "sw"):
            for i in range(self.NDMA):
                s = self.stack.enter_context(nc.semaphore("d%s%d" % (r, i)))
                self.dsem[r].append([self._key(), s, 0, None])
        self.dnext = {"hw": 0, "sw": 0}
        self.out_tokens = []
        self.ninstr = 0
        return self

    def __exit__(self, *a):
        return self.stack.__exit__(*a)

    def _key(self):
        self.nsem += 1
        return self.nsem

    def _new_esem(self, n):
        if n in self.esem and self.ecnt[n] > 0:
            k0, s0 = self.esem[n]
            self.old_tokens.append((k0, s0, self.ecnt[n], n))
        s = self.semstack.enter_context(self.nc.semaphore("e%s%d" % (n, self.nsem)))
        self.esem[n] = (self._key(), s)
        self.ecnt[n] = 0

    def sb(self, name, shape, dt):
        self.ntens = getattr(self, "ntens", 0) + 1
        t = self.stack.enter_context(self.nc.sbuf_tensor("sb%d_%s" % (self.ntens, name), list(shape), dt))
        return Buf(t, name)

    def ps(self, name, shape, dt=F32):
        self.ntens = getattr(self, "ntens", 0) + 1
        t = self.stack.enter_context(self.nc.psum_tensor("ps%d_%s" % (self.ntens, name), list(shape), dt))
        return Buf(t, name)

    def _wait(self, en, tok):
        if tok is None:
            return
        key, sem, val, src_en = tok
        if src_en == en == "tensor":
            return
        w = self.waited[en]
        if w.get(key, 0) >= val:
            return
        self.eng[en].wait_ge(sem, val)
        self.ninstr += 1
        w[key] = val

    def _deps(self, en, rd, wr):
        for b in rd:
            self._wait(en, b.w)
        for b in wr:
            self._wait(en, b.w)
            for e2, tok in b.r.items():
                self._wait(en, tok)

    def _mark(self, en, tok, rd, wr):
        for b in rd:
            b.r[en] = tok
        for b in wr:
            b.w = tok
            b.r = {}

    def op(self, en, method, *args, rd=(), wr=(), **kw):
        self._deps(en, rd, wr)
        ins = getattr(self.eng[en], method)(*args, **kw)
        key, sem = self.esem[en]
        ins.then_inc(sem, 1)
        self.ecnt[en] += 1
        self.ninstr += 1
        tok = (key, sem, self.ecnt[en], en)
        self._mark(en, tok, rd, wr)
        if self.ecnt[en] >= self.EPOCH:
            self._new_esem(en)
        return tok

    def dma(self, q, dst_buf, out_ap, in_ap, rd=(), out=False, **kw):
        if dst_buf is None:
            wr = []
        elif isinstance(dst_buf, (list, tuple)):
            wr = list(dst_buf)
        else:
            wr = [dst_buf]
        self._deps(q, rd, wr)
        ring = "sw" if q == "gpsimd" else "hw"
        d = self.dsem[ring][self.dnext[ring]]
        self.dnext[ring] = (self.dnext[ring] + 1) % self.NDMA
        self._wait(q, d[3])
        ins = self.eng[q].dma_start(out=out_ap, in_=in_ap, **kw)
        d[2] += 16
        ins.then_inc(d[1], 16)
        self.ninstr += 1
        tok = (d[0], d[1], d[2], "dma")
        d[3] = tok
        for b in rd:
            b.r["dma%d" % d[0]] = tok
        for b in wr:
            b.w = tok
            b.r = {}
        if out:
            self.out_tokens.append(tok)
        return tok


    def barrier(self):
        toks = []
        for n in self.ENGS:
            if self.ecnt[n] > 0:
                key, sem = self.esem[n]
                toks.append((key, sem, self.ecnt[n], n))
        toks += [d[3] for r in self.dsem.values() for d in r if d[3] is not None]
        toks += self.old_tokens
        for n in self.ENGS:
            for t in toks:
                self._wait(n, t)

    class _Scope:
        def __init__(self, kb):
            self.kb = kb
        def __enter__(self):
            self.saved = self.kb.stack
            self.kb.stack = ExitStack()
            self.kb.stack.__enter__()
            return self
        def __exit__(self, *a):
            if a[0] is None:
                self.kb.barrier()
            r = self.kb.stack.__exit__(*a)
            self.kb.stack = self.saved
            return r

    def scope(self):
        return KB._Scope(self)

    def finish(self):
        for tok in self.out_tokens:
            self._wait("sync", tok)
        for r in self.dsem.values():
            for d in r:
                self._wait("sync", d[3])


NCORE = 8
NB = 4
NCTX = 256
NLAT = 2048
TOK = NB * (NCTX + NLAT)
D = 1024
DFF = 2816
ALPHA = 4.0 ** 0.25
EPS_LN = 1e-5 / (ALPHA * ALPHA)
NBLK = 256


def ctx0(b):
    return b * NCTX


def lat0(b):
    return NB * NCTX + b * NLAT


def VC_ADAB(l):
    return l * 72


def VC_LNG(l, j):
    return 144 + (l * 3 + j) * 8


def VC_LNB(l, j):
    return 192 + (l * 3 + j) * 8


VC_QG = 240
VC_KG = 241
VC_CW = 242
VC_CB = 266
NVEC = 274


def all_blocks():
    blks = []
    for b in range(NB):
        blks.append((ctx0(b), 4))
    for b in range(NB):
        for i in range(NLAT // NBLK):
            blks.append((lat0(b) + i * NBLK, b))
    return blks


def fm(ap):
    return ap.rearrange("(k p) t -> p k t", p=128)


def prologue(kb, io, G):
    modsT, vecs = G["modsT"], G["vecs"]
    kb.dma("sync", vecs, vecs.t[:], io["vecs"][:, :])
    kb.op("gpsimd", "memset", G["ones_ln"].t[:], 1.0 / D, wr=[G["ones_ln"]])
    with kb.scope():
        cT = kb.sb("cT", [128, 8, 6], F32)
        scT = kb.sb("scT", [128, 8, 6], F32)
        kb.dma("sync", cT, cT.t[:], io["cT"][:, :, :])
        kb.op("scalar", "activation", scT.t[:], cT.t[:], AF.Silu, rd=[cT], wr=[scT])
        aw = [kb.sb("aw%d" % i, [128, 8, 1152], F32) for i in range(2)]
        ps = [kb.ps("pps%d" % i, [128, 512]) for i in range(2)]
        n = 0
        for l in range(2):
            awl = io["ada_w"][l].rearrange("(k p) n -> p k n", p=128)
            for piece in range(8):
                a = aw[n % 2]
                n += 1
                kb.dma("sync", a, a.t[:], awl[:, :, piece * 1152:(piece + 1) * 1152])
                for f in range(9):
                    fc = piece * 9 + f
                    p = ps[fc % 2]
                    for k in range(8):
                        kb.op("tensor", "matmul", p.t[:, 0:6], a.t[:, k, f * 128:(f + 1) * 128], scT.t[:, k, :],
                              start=(k == 0), stop=(k == 7), rd=[a, scT], wr=[p])
                    kb.op("vector", "tensor_scalar", modsT.t[:, l, fc, :], p.t[:, 0:6],
                          vecs.t[:, VC_ADAB(l) + fc:VC_ADAB(l) + fc + 1], None, ALU.add,
                          rd=[p, vecs], wr=[modsT])
        for l in range(2):
            for m in (1, 4, 7):
                kb.op("vector", "tensor_scalar", modsT.t[:, l, m * 8:(m + 1) * 8, :], modsT.t[:, l, m * 8:(m + 1) * 8, :],
                      1.0, None, ALU.add, rd=[modsT], wr=[modsT])
            for m, c in ((2, 0.5 / ALPHA), (8, 0.5 / ALPHA), (5, 1.0 / ALPHA)):
                kb.op("vector", "tensor_scalar", modsT.t[:, l, m * 8:(m + 1) * 8, :], modsT.t[:, l, m * 8:(m + 1) * 8, :],
                      c, None, ALU.mult, rd=[modsT], wr=[modsT])


def ln_tail(kb, G, hb, hbc, sq, sqb, T, N, l, lnj, pst):
    vecs, ones = G["vecs"], G["ones_ln"]
    pm, pq = pst
    if isinstance(pm, tuple):
        (pm, pma), (pq, pqa) = pm, pq
    else:
        pma, pqa = pm.t[:, 0:N], pq.t[:, 0:N]
    for m in range(8):
        kb.op("tensor", "matmul", pma, ones.t[:], hb.t[:, m, 0:N], start=(m == 0), stop=(m == 7),
              rd=[ones, hbc[m]], wr=[pm])
    for m in range(8):
        kb.op("tensor", "matmul", pqa, ones.t[:], sq.t[:, m, 0:N], start=(m == 0), stop=(m == 7),
              rd=[ones, sqb[m]], wr=[pq])
    mean, tmp, sd, rstd = T["mean"], T["tmp"], T["sd"], T["rstd"]
    kb.op("scalar", "activation", mean.t[:, 0:N], pma, AF.Identity, rd=[pm], wr=[mean])
    kb.op("vector", "tensor_tensor", tmp.t[:, 0:N], mean.t[:, 0:N], mean.t[:, 0:N], ALU.mult, rd=[mean], wr=[tmp])
    kb.op("vector", "tensor_tensor", tmp.t[:, 0:N], pqa, tmp.t[:, 0:N], ALU.subtract, rd=[pq, tmp], wr=[tmp])
    kb.op("vector", "tensor_scalar", tmp.t[:, 0:N], tmp.t[:, 0:N], EPS_LN, None, ALU.add, rd=[tmp], wr=[tmp])
    kb.op("scalar", "activation", sd.t[:, 0:N], tmp.t[:, 0:N], AF.Sqrt, rd=[tmp], wr=[sd])
    kb.op("vector", "reciprocal", rstd.t[:, 0:N], sd.t[:, 0:N], rd=[sd], wr=[rstd])
    for m in range(8):
        kb.op("vector", "tensor_tensor", sq.t[:, m, 0:N], hb.t[:, m, 0:N], mean.t[:, 0:N], ALU.subtract,
              rd=[hbc[m], mean], wr=[sqb[m]])
        kb.op("vector", "tensor_tensor", sq.t[:, m, 0:N], sq.t[:, m, 0:N], rstd.t[:, 0:N], ALU.mult,
              rd=[sqb[m], rstd], wr=[sqb[m]])
        cg = VC_LNG(l, lnj) + m
        cb = VC_LNB(l, lnj) + m
        kb.op("scalar", "activation", hb.t[:, m, 0:N], sq.t[:, m, 0:N], AF.Identity,
              bias=vecs.t[:, cb:cb + 1], scale=vecs.t[:, cg:cg + 1], rd=[sqb[m], vecs], wr=[hbc[m]])


def ffn_stage(kb, io, G, src, dst, l, j, blocks, dst_off=0, final=False):
    N = NBLK
    modsT, vecs = G["modsT"], G["vecs"]
    w_in = io["ffn_w_in"][l, j]
    w_out = io["ffn_w_out"][l, j]
    m_sh, m_sc, m_g = (0, 1, 2) if j == 0 else (6, 7, 8)
    lnj = 0 if j == 0 else 2
    srcv, dstv = fm(src), fm(dst)
    with kb.scope():
        win = [kb.sb("win%d" % k, [128, 2 * DFF], BF16) for k in range(8)]
        wout = [kb.sb("wout%d" % q, [128, D], BF16) for q in range(22)]
        for k in range(8):
            kb.dma("gpsimd", win[k], win[k].t[:], w_in[k * 128:(k + 1) * 128, :])
        for q in range(22):
            kb.dma("gpsimd", wout[q], wout[q].t[:], w_out[q * 128:(q + 1) * 128, :])
        hb = [kb.sb("hb%d" % i, [128, 8, N], F32) for i in range(3)]
        hbc = [[Buf() for _ in range(8)] for _ in range(3)]
        xm = [kb.sb("xm%d" % i, [128, 8, N], BF16) for i in range(2)]
        xmb = [[Buf() for _ in range(8)] for _ in range(2)]
        g = kb.sb("g", [128, 22, N], BF16)
        gb = [Buf() for _ in range(22)]
        sq = kb.sb("sq", [128, 8, N], F32)
        sqb = [Buf() for _ in range(8)]
        sa = [kb.sb("sa%d" % i, [128, N], F32) for i in range(2)]
        T = {n: kb.sb("T" + n, [128, N], F32) for n in ("mean", "tmp", "sd", "rstd")}
        psa = [kb.ps("psa%d" % i, [128, 512]) for i in range(2)]
        psu = [kb.ps("psu%d" % i, [128, 512]) for i in range(2)]
        psy = [kb.ps("psy%d" % i, [128, 512]) for i in range(2)]
        pst = [kb.ps("pst%d" % i, [128, 512]) for i in range(2)]

        def load(i):
            t0, col = blocks[i]
            kb.dma("sync", hbc[i % 3], hb[i % 3].t[:], srcv[:, :, t0:t0 + N])

        def A(i):
            t0, col = blocks[i]
            h, hc = hb[i % 3], hbc[i % 3]
            x, xb = xm[i % 2], xmb[i % 2]
            if i + 1 < len(blocks):
                load(i + 1)
            for k in range(8):
                kb.op("scalar", "activation", x.t[:, k, :], h.t[:, k, :], AF.Identity,
                      bias=modsT.t[:, l, m_sh * 8 + k, col:col + 1], scale=modsT.t[:, l, m_sc * 8 + k, col:col + 1],
                      rd=[hc[k], modsT], wr=[xb[k]])
            for jj in range(22):
                pa, pu = psa[jj % 2], psu[jj % 2]
                for k in range(8):
                    kb.op("tensor", "matmul", pa.t[:, 0:N], win[k].t[:, jj * 128:(jj + 1) * 128], x.t[:, k, :],
                          start=(k == 0), stop=(k == 7), rd=[win[k], xb[k]], wr=[pa])
                for k in range(8):
                    kb.op("tensor", "matmul", pu.t[:, 0:N], win[k].t[:, DFF + jj * 128:DFF + (jj + 1) * 128], x.t[:, k, :],
                          start=(k == 0), stop=(k == 7), rd=[win[k], xb[k]], wr=[pu])
                s = sa[jj % 2]
                kb.op("scalar", "activation", s.t[:], pa.t[:, 0:N], AF.Silu, rd=[pa], wr=[s])
                kb.op("vector", "tensor_tensor", g.t[:, jj, :], pu.t[:, 0:N], s.t[:], ALU.mult, rd=[pu, s], wr=[gb[jj]])

        def B(i):
            t0, col = blocks[i]
            h, hc = hb[i % 3], hbc[i % 3]
            for m in range(8):
                py = psy[m % 2]
                for q in range(22):
                    kb.op("tensor", "matmul", py.t[:, 0:N], wout[q].t[:, m * 128:(m + 1) * 128], g.t[:, q, :],
                          start=(q == 0), stop=(q == 21), rd=[wout[q], gb[q]], wr=[py])
                kb.op("vector", "scalar_tensor_tensor", h.t[:, m, :], py.t[:, 0:N],
                      modsT.t[:, l, m_g * 8 + m, col:col + 1], h.t[:, m, :], ALU.mult, ALU.add,
                      rd=[py, hc[m], modsT], wr=[hc[m]])
                kb.op("scalar", "activation", sq.t[:, m, :], h.t[:, m, :], AF.Square, rd=[hc[m]], wr=[sqb[m]])

        def C(i):
            t0, col = blocks[i]
            h, hc = hb[i % 3], hbc[i % 3]
            ln_tail(kb, G, h, hc, sq, sqb, T, N, l, lnj, pst)
            kb.dma("gpsimd", None, dstv[:, :, t0 + dst_off:t0 + dst_off + N], h.t[:], rd=hc, out=final)

        load(0)
        for i in range(len(blocks)):
            A(i)
            if i > 0:
                C(i - 1)
            B(i)
        C(len(blocks) - 1)


def host_consts():
    t = np.arange(NLAT)
    row = (t // 64).astype(np.float32)
    colp = (t % 64).astype(np.float32)
    inv = (np.float32(10000.0) ** (-np.arange(32, dtype=np.float32) / np.float32(32))).astype(np.float32)
    rope = np.zeros((128, 2, NLAT), np.float32)
    for d in range(128):
        a, bb, p = d // 64, (d // 32) % 2, d % 32
        ang = ((row if a == 0 else colp) * inv[p]).astype(np.float32)
        rope[d, 0] = np.cos(ang)
        rope[d, 1] = np.sin(ang) * (-1.0 if bb == 0 else 1.0)
    cm = np.zeros((128, 8, 128), np.float32)
    idx = np.arange(128)
    cm[idx ^ 32, 0, idx] = 1.0
    cm[:, 1, :] = 1.0 / 128.0
    triu = (idx[:, None] <= idx[None, :]).astype(np.float32)
    tril = (idx[:, None] >= idx[None, :]).astype(np.float32)
    cm[:, 2, :] = triu * (128.0 ** -0.5)
    cm[:, 3, :] = tril * (128.0 ** -0.5)
    cm[:, 4, :] = triu
    cm[:, 5, :] = tril
    cm[:, 6, :] = 1.0
    cm[:, 7, :] = np.eye(128, dtype=np.float32)
    return {"rope": rope, "cmat": cm}


def attn_stage(kb, io, G, src, dst, bs=range(NB)):
    N = NBLK
    N2 = 2 * NBLK
    modsT, vecs = G["modsT"], G["vecs"]
    srcv, dstv = fm(src), fm(dst)
    SC = 128.0 ** -0.5
    with kb.scope():
        wi = [kb.sb("awi%d" % k, [128, 1536], BF16) for k in range(8)]
        wo = [kb.sb("awo%d" % k, [128, D], BF16) for k in range(8)]
        for k in range(8):
            kb.dma("gpsimd", wi[k], wi[k].t[:], io["att_w_in"][0][k * 128:(k + 1) * 128, :])
        for k in range(8):
            kb.dma("gpsimd", wo[k], wo[k].t[:], io["att_w_out"][0][k * 128:(k + 1) * 128, :])
        rope = kb.sb("rope", [128, 2, 2, NLAT], F32)
        kb.dma("sync", rope, rope.t[:, :, 0, :], io["rope"][:, :, :])
        kb.dma("sync", rope, rope.t[:, :, 1, :], io["rope"][:, :, :])
        cm = kb.sb("cm", [128, 8, 128], F32)
        kb.dma("sync", cm, cm.t[:], io["cmat"][:, :, :])
        ones_bf = kb.sb("ones_bf", [128, 128], BF16)
        kb.op("gpsimd", "memset", ones_bf.t[:], 1.0, wr=[ones_bf])
        qT = kb.sb("qT", [128, 9, 8, N], BF16)
        epsT = kb.sb("epsT", [128, 1], F32)
        kb.op("gpsimd", "memset", epsT.t[:], 1e-6, wr=[epsT])
        qTb = [[Buf() for _ in range(9)] for _ in range(8)]
        kT = kb.sb("kT", [128, 2, 2304], BF16)
        kTb = [[Buf() for _ in range(9)] for _ in range(2)]
        vt = kb.sb("vt", [128, 18, 256], BF16)
        vtb = [Buf() for _ in range(18)]
        hz = [kb.sb("hz%d" % i, [128, 8, N], F32) for i in range(2)]
        hzc = [[Buf() for _ in range(8)] for _ in range(2)]
        xm = [kb.sb("axm%d" % i, [128, 8, N], BF16) for i in range(2)]
        xmb = [[Buf() for _ in range(8)] for _ in range(2)]
        W = {n: [kb.sb("aw_%s%d" % (n, i), [128, 2, N], F32) for i in range(2)] for n in ("sqh", "rs", "qg", "t2")}
        sq = kb.sb("asq", [128, 8, N], F32)
        sqb = [Buf() for _ in range(8)]
        attnT = [kb.sb("attnT%d" % i, [128, 8, N], BF16) for i in range(2)]
        atb = [[Buf() for _ in range(8)] for _ in range(2)]
        pT = [kb.sb("pT%d" % i, [128, N2], BF16) for i in range(3)]
        rden = [kb.sb("rden%d" % i, [128, N2], F32) for i in range(2)]
        T = {n: kb.sb("aT" + n, [128, N], F32) for n in ("mean", "tmp", "sd", "rstd")}
        P = [kb.ps("aP%d" % i, [128, 512]) for i in range(8)]
        st = {"hz": 0}

        def blk_info(b, blk):
            is_ctx = blk == 8
            t0 = ctx0(b) if is_ctx else lat0(b) + blk * N
            lc = 2048 if is_ctx else blk * N
            col = 4 if is_ctx else b
            return is_ctx, t0, lc, col

        def load_h(b, blk):
            i = st["hz"] % 2
            st["hz"] += 1
            _, t0, _, _ = blk_info(b, blk)
            kb.dma("sync", hzc[i], hz[i].t[:], srcv[:, :, t0:t0 + N])
            return i

        def v3(t):
            return t.t[:].rearrange("p a n -> p (a n)")

        def proj_block(b, blk, hi, nxt):
            is_ctx, t0, lc, col = blk_info(b, blk)
            h, hc = hz[hi], hzc[hi]
            x, xb = xm[hi], xmb[hi]
            nhi = load_h(*nxt) if nxt is not None else None
            for k in range(8):
                kb.op("scalar", "activation", x.t[:, k, :], h.t[:, k, :], AF.Identity,
                      bias=modsT.t[:, 0, 3 * 8 + k, col:col + 1], scale=modsT.t[:, 0, 4 * 8 + k, col:col + 1],
                      rd=[hc[k], modsT], wr=[xb[k]])
            for pr in range(5):
                c0 = pr * 256
                i2 = pr % 2
                pq = P[i2]
                for s in range(2):
                    for k in range(8):
                        kb.op("tensor", "matmul", pq.t[:, s * N:(s + 1) * N], wi[k].t[:, c0 + s * 128:c0 + (s + 1) * 128],
                              x.t[:, k, :], start=(k == 0), stop=(k == 7), rd=[wi[k], xb[k]], wr=[pq])
                sqh, rs, qg, t2 = (W[n][i2] for n in ("sqh", "rs", "qg", "t2"))
                kb.op("scalar", "activation", v3(sqh), pq.t[:, 0:N2], AF.Square, rd=[pq], wr=[sqh])
                gcol = VC_QG if pr < 4 else VC_KG
                kb.op("scalar", "activation", v3(qg), pq.t[:, 0:N2], AF.Identity, scale=vecs.t[:, gcol:gcol + 1],
                      rd=[pq, vecs], wr=[qg])
                pss = P[2 + i2]
                kb.op("tensor", "matmul", pss.t[:, 0:N2], cm.t[:, 1, :], v3(sqh), start=True, stop=True,
                      rd=[cm, sqh], wr=[pss])
                if not is_ctx:
                    pp = P[4 + i2]
                    kb.op("tensor", "matmul", pp.t[:, 0:N2], cm.t[:, 0, :], v3(qg), start=True, stop=True,
                          rd=[cm, qg], wr=[pp])
                kb.op("scalar", "activation", v3(rs), pss.t[:, 0:N2], AF.Sqrt, bias=epsT.t[:, 0:1], rd=[pss, epsT], wr=[rs])
                kb.op("vector", "reciprocal", v3(rs), v3(rs), rd=[rs], wr=[rs])
                if pr < 4:
                    dest, dbufs = qT.t[:, blk, 2 * pr:2 * pr + 2, :], [qTb[2 * pr][blk], qTb[2 * pr + 1][blk]]
                else:
                    dest, dbufs = kT.t[:, 0:2, lc:lc + N], [kTb[0][blk], kTb[1][blk]]
                if not is_ctx:
                    pos = blk * N
                    kb.op("gpsimd", "tensor_tensor", sqh.t[:], qg.t[:], rope.t[:, 0, :, pos:pos + N], ALU.mult,
                          rd=[qg, rope], wr=[sqh])
                    kb.op("vector", "tensor_tensor", t2.t[:], pp.t[:, 0:N2].rearrange("p (a n) -> p a n", a=2),
                          rope.t[:, 1, :, pos:pos + N], ALU.mult, rd=[pp, rope], wr=[t2])
                    kb.op("vector", "tensor_tensor", t2.t[:], t2.t[:], sqh.t[:], ALU.add, rd=[t2, sqh], wr=[t2])
                    kb.op("vector", "tensor_tensor", dest, t2.t[:], rs.t[:], ALU.mult, rd=[t2, rs], wr=dbufs)
                else:
                    kb.op("vector", "tensor_tensor", dest, qg.t[:], rs.t[:], ALU.mult, rd=[qg, rs], wr=dbufs)
            for tl in range(2):
                pv = P[6 + tl]
                for k in range(8):
                    kb.op("tensor", "matmul", pv.t[:, 0:256], x.t[:, k, tl * 128:(tl + 1) * 128], wi[k].t[:, 1280:1536],
                          start=(k == 0), stop=(k == 7), rd=[xb[k], wi[k]], wr=[pv])
                kt = lc // 128 + tl
                kb.op("scalar", "activation", vt.t[:, kt, :], pv.t[:, 0:256], AF.Identity, rd=[pv], wr=[vtb[kt]])
            return nhi

        def attention(b):
            units = []
            for qb in range(9):
                kts = list(range(18)) if qb < 8 else [16, 17]
                for hp in range(4):
                    for j, kt in enumerate(kts):
                        units.append((qb, hp, kt, j == 0, j == len(kts) - 1))
            hz_of = {}
            P7 = P[7]

            def S(u, i):
                qb, hp, kt, first, last = u
                ps = P[i % 3]
                kb.op("tensor", "matmul", ps.t[:, 0:N2], kT.t[:, hp // 2, kt * 128:(kt + 1) * 128],
                      qT.t[:, qb, 2 * hp:2 * hp + 2, :].rearrange("p a n -> p (a n)"), start=True, stop=True,
                      rd=[kTb[hp // 2][kt // 2], qTb[2 * hp][qb], qTb[2 * hp + 1][qb]], wr=[ps])

            def tail1(qb):
                _, t0, lc, col = blk_info(b, qb)
                hi = hz_of[qb]
                h, hc = hz[hi], hzc[hi]
                at, ab = attnT[qb % 2], atb[qb % 2]
                for m in range(8):
                    py = P7
                    for hh in range(8):
                        kb.op("tensor", "matmul", py.t[:, 0:N], wo[hh].t[:, m * 128:(m + 1) * 128], at.t[:, hh, :],
                              start=(hh == 0), stop=(hh == 7), rd=[wo[hh], ab[hh]], wr=[py])
                    kb.op("vector", "scalar_tensor_tensor", h.t[:, m, :], py.t[:, 0:N],
                          modsT.t[:, 0, 5 * 8 + m, col:col + 1], h.t[:, m, :], ALU.mult, ALU.add,
                          rd=[py, hc[m], modsT], wr=[hc[m]])
                    kb.op("scalar", "activation", sq.t[:, m, :], h.t[:, m, :], AF.Square, rd=[hc[m]], wr=[sqb[m]])

            def tail2(qb):
                _, t0, lc, col = blk_info(b, qb)
                hi = hz_of[qb]
                ln_tail(kb, G, hz[hi], hzc[hi], sq, sqb, T, N, 0, 1, ((P7, P7.t[:, 0:N]), (P7, P7.t[:, N:N2])))
                kb.dma("gpsimd", None, dstv[:, :, t0:t0 + N], hz[hi].t[:], rd=hzc[hi])

            pending = []
            hcount = 0
            LA = 2
            for i, u in enumerate(units):
                qb, hp, kt, first, last = u
                if i == 0:
                    for j in range(min(LA, len(units))):
                        S(units[j], j)
                if first and hp == 0:
                    hz_of[qb] = load_h(b, qb)
                if i + LA < len(units):
                    S(units[i + LA], i + LA)
                ps, p = P[i % 3], pT[i % 3]
                kb.op("scalar", "activation", p.t[:], ps.t[:, 0:N2], AF.Exp, scale=SC, rd=[ps], wr=[p])
                po, pd = P[3 + hcount % 2], P[5 + hcount % 2]
                kv = hp // 2
                kb.op("tensor", "matmul", po.t[:, 0:N2], vt.t[:, kt, kv * 128:(kv + 1) * 128], p.t[:],
                      start=first, stop=last, rd=[vtb[kt], p], wr=[po])
                kb.op("tensor", "matmul", pd.t[:, 0:N2], ones_bf.t[:], p.t[:], start=first, stop=last,
                      rd=[ones_bf, p], wr=[pd])
                if last:
                    rd_ = rden[hcount % 2]
                    kb.op("vector", "reciprocal", rd_.t[:], pd.t[:, 0:N2], rd=[pd], wr=[rd_])
                    kb.op("vector", "tensor_tensor", attnT[qb % 2].t[:, 2 * hp:2 * hp + 2, :].rearrange("p a n -> p (a n)"),
                          po.t[:, 0:N2], rd_.t[:], ALU.mult,
                          rd=[po, rd_], wr=[atb[qb % 2][2 * hp], atb[qb % 2][2 * hp + 1]])
                    hcount += 1
                    if hp == 3:
                        pending.append((i + 3, tail1, qb))
                        pending.append((i + 7, tail2, qb))
                while pending and pending[0][0] <= i:
                    _, fn, a = pending.pop(0)
                    fn(a)
            for _, fn, a in pending:
                fn(a)

        for b in bs:
            order = list(range(9))
            hi = load_h(b, 0)
            for j, blk in enumerate(order):
                nxt = (b, order[j + 1]) if j + 1 < len(order) else None
                hi = proj_block(b, blk, hi, nxt)
            attention(b)


def mlstm_stage(kb, io, G, src, dst, bs=range(NB)):
    N = NBLK
    L1 = 1
    modsT, vecs = G["modsT"], G["vecs"]
    srcv, dstv = fm(src), fm(dst)
    SC = 128.0 ** -0.5
    wml = io["ml_w_in"][0]
    HF, SIG = io["HF"], io["SIG"]
    with kb.scope():
        cm = kb.sb("mcm", [128, 8, 128], F32)
        kb.dma("sync", cm, cm.t[:], io["cmat"][:, :, :])
        ident = kb.sb("ident", [128, 128], BF16)
        kb.op("vector", "tensor_copy", ident.t[:], cm.t[:, 7, :], rd=[cm], wr=[ident])
        ngb = kb.sb("ngb", [128, D], F32)
        kb.dma("sync", ngb, ngb.t[:], io["ngb"][:, :])
        gbb = kb.sb("gbb", [128, 16], F32)
        kb.dma("sync", gbb, gbb.t[:], io["gbb"][:, :])
        qT = kb.sb("mqT", [128, 4, 2304], BF16)
        qTb = [[Buf() for _ in range(9)] for _ in range(4)]
        kT = kb.sb("mkT", [128, 4, 2304], BF16)
        kTb = [[Buf() for _ in range(9)] for _ in range(4)]
        ktok = kb.sb("ktok", [128, 18, 4, 128], BF16)
        ktb = [Buf() for _ in range(18)]
        vext = kb.sb("vext", [128, 18, 4, 258], BF16)
        vxb = [Buf() for _ in range(18)]
        vones = Buf()
        kb.op("gpsimd", "memset", vext.t[:, :, :, 256:257], 1.0, wr=[vones])
        kb.op("gpsimd", "memset", vext.t[:, :, :, 257:258], 0.0, wr=[vones])
        aa = kb.sb("aa", [128, 18, 8], F32)
        bqa = kb.sb("bqa", [128, 18, 8], F32)
        edec = kb.sb("edec", [128, 18, 8], F32)
        nbqa = kb.sb("nbqa", [128, 18, 8], F32)
        scb = [Buf() for _ in range(18)]
        P = [kb.ps("mP%d" % i, [128, 512]) for i in range(6)]
        Pb = kb.ps("mPb", [128, 8, 128], BF16)
        P7 = kb.ps("mP7", [128, 512])

        def blk_info(b, blk):
            t0 = ctx0(b) if blk == 0 else lat0(b) + (blk - 1) * N
            col = 4 if blk == 0 else b
            return t0, blk * N, col

        def phase1(b):
            with kb.scope():
                wqk = [kb.sb("wqk%d" % k, [128, 1024], BF16) for k in range(8)]
                wv = [kb.sb("wv%d" % k, [128, 1024], BF16) for k in range(8)]
                wo_ = [kb.sb("wog%d" % k, [128, 1024], BF16) for k in range(8)]
                wg = [kb.sb("wg%d" % k, [128, 16], BF16) for k in range(8)]
                for k in range(8):
                    rows = slice(k * 128, (k + 1) * 128)
                    kb.dma("gpsimd", wqk[k], wqk[k].t[:], wml[rows, 0:1024])
                    kb.dma("gpsimd", wv[k], wv[k].t[:], wml[rows, 1024:2048])
                    kb.dma("gpsimd", wo_[k], wo_[k].t[:], wml[rows, 2048:3072])
                    kb.dma("gpsimd", wg[k], wg[k].t[:], wml[rows, 3072:3088])
                xh = [kb.sb("xh%d" % i, [128, 8, 258], F32) for i in range(2)]
                xmm = [kb.sb("mxm%d" % i, [128, 8, 258], BF16) for i in range(2)]
                acc = [kb.sb("acc%d" % i, [128, N], F32) for i in range(2)]
                gs = kb.sb("gs", [128, 16], F32)
                lf = kb.sb("lf", [128, 8], F32)
                tmpa = kb.sb("tmpa", [128, 8], F32)
                sigt = [kb.sb("sigt%d" % i, [128, D], F32) for i in range(2)]
                nsig = 0

                def load_x(blk):
                    t0, lc, col = blk_info(b, blk)
                    x = xh[blk % 2]
                    hasl = blk >= 2
                    hasr = 1 <= blk <= 7
                    kb.op("gpsimd", "memset", x.t[:, :, 0:1], 0.0, wr=[x])
                    kb.op("gpsimd", "memset", x.t[:, :, 257:258], 0.0, wr=[x])
                    lo = 0 if hasl else 1
                    hi = 258 if hasr else 257
                    kb.dma("sync", x, x.t[:, :, lo:hi], srcv[:, :, t0 - 1 + lo:t0 - 1 + hi])

                load_x(0)
                for blk in range(9):
                    t0, lc, col = blk_info(b, blk)
                    if blk + 1 < 9:
                        load_x(blk + 1)
                    x, xm_ = xh[blk % 2], xmm[blk % 2]
                    for k in range(8):
                        kb.op("scalar", "activation", xm_.t[:, k, :], x.t[:, k, :], AF.Identity,
                              bias=modsT.t[:, L1, 3 * 8 + k, col:col + 1], scale=modsT.t[:, L1, 4 * 8 + k, col:col + 1],
                              rd=[x, modsT], wr=[xm_])
                    if not blk >= 2:
                        kb.op("gpsimd", "memset", xm_.t[:, :, 0:1], 0.0, wr=[xm_])
                    if not 1 <= blk <= 7:
                        kb.op("gpsimd", "memset", xm_.t[:, :, 257:258], 0.0, wr=[xm_])
                    for c8 in range(8):
                        pqk = P[c8 % 2]
                        for k in range(8):
                            kb.op("tensor", "matmul", pqk.t[:, 0:258], wqk[k].t[:, c8 * 128:(c8 + 1) * 128], xm_.t[:, k, :],
                                  start=(k == 0), stop=(k == 7), rd=[wqk[k], xm_], wr=[pqk])
                        a_ = acc[c8 % 2]
                        w0, w1, w2, cb = (VC_CW + c8, VC_CW + 8 + c8, VC_CW + 16 + c8, VC_CB + c8)
                        kb.op("vector", "tensor_scalar", a_.t[:], pqk.t[:, 0:256], vecs.t[:, w0:w0 + 1], vecs.t[:, cb:cb + 1],
                              ALU.mult, ALU.add, rd=[pqk, vecs], wr=[a_])
                        kb.op("vector", "scalar_tensor_tensor", a_.t[:], pqk.t[:, 1:257], vecs.t[:, w1:w1 + 1], a_.t[:],
                              ALU.mult, ALU.add, rd=[pqk, vecs, a_], wr=[a_])
                        kb.op("vector", "scalar_tensor_tensor", a_.t[:], pqk.t[:, 2:258], vecs.t[:, w2:w2 + 1], a_.t[:],
                              ALU.mult, ALU.add, rd=[pqk, vecs, a_], wr=[a_])
                        if c8 < 4:
                            dest, dbuf = qT.t[:, c8, lc:lc + N], qTb[c8][blk]
                        else:
                            dest, dbuf = kT.t[:, c8 - 4, lc:lc + N], kTb[c8 - 4][blk]
                        kb.op("scalar", "activation", dest, a_.t[:], AF.Silu, rd=[a_], wr=[dbuf])
                    for tl in range(2):
                        c = 2 * blk + tl
                        xs = slice(1 + tl * 128, 1 + (tl + 1) * 128)
                        for h in range(4):
                            kb.op("tensor", "transpose", Pb.t[:, h, :], kT.t[:, h, lc + tl * 128:lc + (tl + 1) * 128], ident.t[:],
                                  rd=[kTb[h][blk], ident], wr=[Pb])
                        kb.op("scalar", "activation", ktok.t[:, c, :, :], Pb.t[:, 0:4, :], AF.Identity, rd=[Pb], wr=[ktb[c]])
                        pg = P[4]
                        for k in range(8):
                            kb.op("tensor", "matmul", pg.t[:, 0:16], xm_.t[:, k, xs], wg[k].t[:, 0:16],
                                  start=(k == 0), stop=(k == 7), rd=[xm_, wg[k]], wr=[pg])
                        kb.op("vector", "tensor_tensor", gs.t[:], pg.t[:, 0:16], gbb.t[:], ALU.add, rd=[pg, gbb], wr=[gs])
                        gsv = gs.t[:].rearrange("p (d t h) -> p d t h", d=2, t=2)
                        lfv = lf.t[:].rearrange("p (d h) -> p d h", d=2)
                        kb.op("scalar", "activation", lfv, gsv[:, :, 1, :], AF.Exp, scale=-1.0, rd=[gs], wr=[lf])
                        kb.op("vector", "tensor_scalar", lf.t[:], lf.t[:], 1.0, None, ALU.add, rd=[lf], wr=[lf])
                        kb.op("scalar", "activation", lf.t[:], lf.t[:], AF.Ln, rd=[lf], wr=[lf])
                        pc = P[5]
                        kb.op("tensor", "matmul", pc.t[:, 0:4], cm.t[:, 4, :], lf.t[:, 0:4], start=True, stop=True,
                              rd=[cm, lf], wr=[pc])
                        kb.op("tensor", "matmul", pc.t[:, 4:8], cm.t[:, 5, :], lf.t[:, 4:8], start=True, stop=True,
                              rd=[cm, lf], wr=[pc])
                        kb.op("tensor", "matmul", pc.t[:, 8:16], cm.t[:, 6, :], lf.t[:, 0:8], start=True, stop=True,
                              rd=[cm, lf], wr=[pc])
                        tav = tmpa.t[:].rearrange("p (d h) -> p d h", d=2)
                        pcv = pc.t[:, 0:8].rearrange("p (d h) -> p d h", d=2)
                        kb.op("vector", "tensor_tensor", tav, pcv, gsv[:, :, 0, :], ALU.add, rd=[pc, gs], wr=[tmpa])
                        kb.op("scalar", "activation", bqa.t[:, c, :], pc.t[:, 0:8], AF.Exp, scale=-1.0, rd=[pc, tmpa], wr=[scb[c]])
                        kb.op("scalar", "activation", edec.t[:, c, :], pc.t[:, 8:16], AF.Exp, scale=-1.0, rd=[pc], wr=[scb[c]])
                        kb.op("vector", "tensor_scalar", nbqa.t[:, c, :], bqa.t[:, c, :], -1.0, None, ALU.mult, rd=[scb[c]], wr=[scb[c]])
                        kb.op("scalar", "activation", aa.t[:, c, :], tmpa.t[:], AF.Exp, rd=[tmpa], wr=[scb[c]])
                        for half in range(2):
                            pv = P[2 + half]
                            for k in range(8):
                                kb.op("tensor", "matmul", pv.t[:, 0:512], xm_.t[:, k, xs], wv[k].t[:, half * 512:(half + 1) * 512],
                                      start=(k == 0), stop=(k == 7), rd=[xm_, wv[k]], wr=[pv])
                            kb.op("scalar", "activation", vext.t[:, c, 2 * half:2 * half + 2, 0:256],
                                  pv.t[:, 0:512].rearrange("p (h v) -> p h v", h=2), AF.Identity, rd=[pv], wr=[vxb[c]])
                        if blk >= 1:
                            sg = sigt[nsig % 2]
                            nsig += 1
                            for half in range(2):
                                po = P[2 + half]
                                for k in range(8):
                                    kb.op("tensor", "matmul", po.t[:, 0:512], xm_.t[:, k, xs], wo_[k].t[:, half * 512:(half + 1) * 512],
                                          start=(k == 0), stop=(k == 7), rd=[xm_, wo_[k]], wr=[po])
                                kb.op("scalar", "activation", sg.t[:, half * 512:(half + 1) * 512], po.t[:, 0:512], AF.Sigmoid,
                                      rd=[po], wr=[sg])
                            kb.dma("gpsimd", G["sigd"][b][c - 2], SIG[b, c - 2], sg.t[:], rd=[sg])

        def scan(b, direction):
            fwd = direction == 0
            order = list(range(18)) if fwd else [1, 0] + list(range(17, 1, -1))
            dcol = 0 if fwd else 4
            mask = cm.t[:, 2, :] if fwd else cm.t[:, 3, :]
            with kb.scope():
                cext = kb.sb("cext", [128, 4, 258], F32)
                cb_ = kb.sb("cb", [128, 4, 258], BF16)
                cxb = [Buf() for _ in range(4)]
                cbb = [Buf() for _ in range(4)]
                kb.op("gpsimd", "memset", cext.t[:], 0.0, wr=cxb)
                kb.op("gpsimd", "memset", cb_.t[:], 0.0, wr=cbb)
                sm = [kb.sb("sm%d" % i, [128, 128], BF16) for i in range(2)]
                ka = [kb.sb("ka%d" % i, [128, 128], BF16) for i in range(2)]
                tmpc = [kb.sb("tmpc%d" % i, [128, 258], F32) for i in range(2)]
                dd = [kb.sb("dd%d" % i, [128, 4], F32) for i in range(2)]
                hsum = [kb.sb("hsum%d" % i, [128, D], F32) for i in range(3)]
                if not fwd:
                    wout = [kb.sb("mwo%d" % k, [128, D], BF16) for k in range(8)]
                    for k in range(8):
                        kb.dma("gpsimd", wout[k], wout[k].t[:], io["ml_w_out"][0][k * 128:(k + 1) * 128, :])
                    hfl = [kb.sb("hfl%d" % i, [128, D], F32) for i in range(2)]
                    sgl = [kb.sb("sgl%d" % i, [128, D], F32) for i in range(2)]
                    rbf = kb.sb("rbf", [128, D], BF16)
                    rT = [kb.sb("rT%d" % i, [128, 8, N], BF16) for i in range(2)]
                    rTb = [[Buf() for _ in range(2)] for _ in range(2)]
                    h4 = [kb.sb("h4%d" % i, [128, 8, N], F32) for i in range(2)]
                    h4c = [[Buf() for _ in range(8)] for _ in range(2)]
                    sq = kb.sb("msq", [128, 8, N], F32)
                    sqb = [Buf() for _ in range(8)]
                    T = {n: kb.sb("mT" + n, [128, N], F32) for n in ("mean", "tmp", "sd", "rstd")}
                    st6s = [kb.sb("st6%d" % i, [128, 4, 6], F32) for i in range(2)]
                    mvs = [kb.sb("mv%d" % i, [128, 4, 2], F32) for i in range(2)]
                    rs4s = [kb.sb("rs4%d" % i, [128, 4], F32) for i in range(2)]

                units = [(c, h) for c in order for h in range(4)]
                latu = [u for u in units if u[0] >= 2]

                def S(u, j):
                    c, h = u
                    ps = P[j % 2]
                    cs = slice(c * 128, (c + 1) * 128)
                    kb.op("tensor", "matmul", ps.t[:, 0:128], kT.t[:, h, cs], qT.t[:, h, cs], start=True, stop=True,
                          rd=[kTb[h][c // 2], qTb[h][c // 2]], wr=[ps])

                lat_order = [c for c in order if c >= 2]
                jidx = {c: j for j, c in enumerate(lat_order)}

                def prefetch_hfl(c):
                    kb.dma("sync", hfl[c % 2], hfl[c % 2].t[:], HF[b, c - 2], rd=[G["hfd"][b][c - 2]])

                def prefetch_sgl(c):
                    kb.dma("sync", sgl[c % 2], sgl[c % 2].t[:], SIG[b, c - 2], rd=[G["sigd"][b][c - 2]])

                def load_h4(bi):
                    i = bi % 2
                    t0 = lat0(b) + bi * N
                    kb.dma("sync", h4c[i], h4[i].t[:], srcv[:, :, t0:t0 + N])

                def R1a(c):
                    j = jidx[c]
                    hs = hsum[j % 3]
                    st6, mv, rs4 = st6s[j % 2], mvs[j % 2], rs4s[j % 2]
                    for h in range(4):
                        kb.op("vector", "bn_stats", st6.t[:, h, :], hs.t[:, h * 256:(h + 1) * 256], rd=[hs], wr=[st6])
                    for h in range(4):
                        kb.op("vector", "bn_aggr", mv.t[:, h, :], st6.t[:, h, :], rd=[st6], wr=[mv])
                    kb.op("vector", "tensor_scalar", rs4.t[:], mv.t[:, :, 1], 1e-5, None, ALU.add, rd=[mv], wr=[rs4])
                    kb.op("scalar", "activation", rs4.t[:], rs4.t[:], AF.Sqrt, rd=[rs4], wr=[rs4])

                def R1b(c):
                    j = jidx[c]
                    hs = hsum[j % 3]
                    st6, mv, rs4 = st6s[j % 2], mvs[j % 2], rs4s[j % 2]
                    kb.op("vector", "reciprocal", rs4.t[:], rs4.t[:], rd=[rs4], wr=[rs4])
                    for h in range(4):
                        kb.op("vector", "tensor_scalar", hs.t[:, h * 256:(h + 1) * 256], hs.t[:, h * 256:(h + 1) * 256],
                              mv.t[:, h, 0:1], rs4.t[:, h:h + 1], ALU.subtract, ALU.mult, rd=[hs, mv, rs4], wr=[hs])
                    kb.op("gpsimd", "tensor_tensor", hs.t[:], hs.t[:], ngb.t[:], ALU.mult, rd=[hs, ngb], wr=[hs])
                    kb.op("gpsimd", "tensor_tensor", rbf.t[:], hs.t[:], sgl[c % 2].t[:], ALU.mult, rd=[hs, sgl[c % 2]], wr=[rbf])
                    for k in range(8):
                        kb.op("tensor", "transpose", Pb.t[:, k, :], rbf.t[:, k * 128:(k + 1) * 128], ident.t[:],
                              rd=[rbf, ident], wr=[Pb])
                    half = (c - 2) % 2
                    bi = (c - 2) // 2
                    kb.op("scalar", "activation", rT[bi % 2].t[:, :, half * 128:(half + 1) * 128], Pb.t[:, :, :], AF.Identity,
                          rd=[Pb], wr=[rTb[bi % 2][half]])

                def T1(bi):
                    i = bi % 2
                    h, hc = h4[i], h4c[i]
                    for m in range(8):
                        py = P7
                        for k in range(8):
                            kb.op("tensor", "matmul", py.t[:, 0:N], wout[k].t[:, m * 128:(m + 1) * 128], rT[i].t[:, k, :],
                                  start=(k == 0), stop=(k == 7), rd=[wout[k]] + rTb[i], wr=[py])
                        kb.op("vector", "scalar_tensor_tensor", h.t[:, m, :], py.t[:, 0:N],
                              modsT.t[:, L1, 5 * 8 + m, b:b + 1], h.t[:, m, :], ALU.mult, ALU.add,
                              rd=[py, hc[m], modsT], wr=[hc[m]])
                        kb.op("scalar", "activation", sq.t[:, m, :], h.t[:, m, :], AF.Square, rd=[hc[m]], wr=[sqb[m]])

                def T2(bi):
                    i = bi % 2
                    h, hc = h4[i], h4c[i]
                    t0 = lat0(b) + bi * N
                    ln_tail(kb, G, h, hc, sq, sqb, T, N, L1, 1, ((P7, P7.t[:, 0:N]), (P7, P7.t[:, N:2 * N])))
                    kb.dma("gpsimd", None, dstv[:, :, t0:t0 + N], h.t[:], rd=hc)
                    if bi - 2 >= 0:
                        load_h4(bi - 2)

                sched = {}

                def end_chunk_bwd(c):
                    j = jidx[c]
                    if j + 2 < 16:
                        prefetch_hfl(lat_order[j + 2])
                    if j >= 1:
                        R1b(lat_order[j - 1])
                        if j + 1 < 16:
                            prefetch_sgl(lat_order[j + 1])
                    R1a(c)
                    for fn, a in sched.pop(j, []):
                        fn(a)
                    if c % 2 == 0:
                        bi = (c - 2) // 2
                        sched.setdefault(j + 2, []).append((T1, bi))
                        sched.setdefault(j + 3, []).append((T2, bi))

                def flush_bwd():
                    R1b(lat_order[15])
                    for j in sorted(sched):
                        for fn, a in sched[j]:
                            fn(a)
                    sched.clear()

                if not fwd:
                    prefetch_hfl(17)
                    prefetch_hfl(16)
                    prefetch_sgl(17)
                    prefetch_sgl(16)
                    load_h4(7)
                    load_h4(6)
                jl = 0
                if latu:
                    pass
                first_lat_emitted = False
                for n, (c, h) in enumerate(units):
                    is_lat = c >= 2
                    last_chunk = (c == order[-1])
                    a_ap = aa.t[:, c, dcol + h:dcol + h + 1]
                    bq_ap = bqa.t[:, c, dcol + h:dcol + h + 1]
                    e_ap = edec.t[:, c, dcol + h:dcol + h + 1]
                    nbq_ap = nbqa.t[:, c, dcol + h:dcol + h + 1]
                    if not last_chunk:
                        k_ = ka[n % 2]
                        kb.op("scalar", "activation", k_.t[:], ktok.t[:, c, h, :], AF.Identity, scale=a_ap,
                              rd=[ktb[c], scb[c]], wr=[k_])
                        pu = P[4 + n % 2]
                        kb.op("tensor", "matmul", pu.t[:, 0:257], k_.t[:], vext.t[:, c, h, 0:257], start=True, stop=True,
                              rd=[k_, vxb[c], vones], wr=[pu])
                    if is_lat:
                        if not first_lat_emitted:
                            S((c, h), jl)
                            first_lat_emitted = True
                        if jl + 1 < len(latu):
                            S(latu[jl + 1], jl + 1)
                        ps = P[jl % 2]
                        s_ = sm[jl % 2]
                        kb.op("vector", "scalar_tensor_tensor", s_.t[:], ps.t[:, 0:128], a_ap, mask, ALU.mult, ALU.mult,
                              rd=[ps, scb[c], cm], wr=[s_])
                        pout = P[2 + jl % 2]
                        cs = slice(c * 128, (c + 1) * 128)
                        kb.op("tensor", "matmul", pout.t[:, 0:257], s_.t[:], vext.t[:, c, h, 0:257], start=True, stop=False,
                              rd=[s_, vxb[c], vones], wr=[pout])
                        kb.op("tensor", "matmul", pout.t[:, 0:257], qT.t[:, h, cs], cb_.t[:, h, 0:257], start=False, stop=True,
                              rd=[qTb[h][c // 2], cbb[h]], wr=[pout])
                    if not last_chunk:
                        tc_ = tmpc[n % 2]
                        kb.op("vector", "scalar_tensor_tensor", tc_.t[:, 0:257], pu.t[:, 0:257], SC, cext.t[:, h, 0:257],
                              ALU.mult, ALU.add, rd=[pu, cxb[h]], wr=[tc_])
                        kb.op("vector", "tensor_scalar", cext.t[:, h, 0:257], tc_.t[:, 0:257], e_ap, None, ALU.mult,
                              rd=[tc_, scb[c]], wr=[cxb[h]])
                        kb.op("scalar", "activation", cb_.t[:, h, 0:257], cext.t[:, h, 0:257], AF.Identity,
                              rd=[cxb[h]], wr=[cbb[h]])
                    if is_lat:
                        d = dd[jl % 2]
                        kb.op("vector", "tensor_scalar", d.t[:, 0:1], pout.t[:, 256:257], bq_ap, 1.0, ALU.mult, ALU.max,
                              rd=[pout, scb[c]], wr=[d])
                        kb.op("vector", "tensor_scalar", d.t[:, 1:2], pout.t[:, 256:257], nbq_ap, 1.0, ALU.mult, ALU.max,
                              rd=[pout, scb[c]], wr=[d])
                        kb.op("vector", "tensor_tensor", d.t[:, 0:1], d.t[:, 0:1], d.t[:, 1:2], ALU.max, rd=[d], wr=[d])
                        kb.op("vector", "reciprocal", d.t[:, 1:2], d.t[:, 0:1], rd=[d], wr=[d])
                        kb.op("vector", "tensor_tensor", d.t[:, 2:3], d.t[:, 1:2], bq_ap, ALU.mult, rd=[d, scb[c]], wr=[d])
                        hs = hsum[c % 2] if fwd else hsum[jidx[c] % 3]
                        if fwd:
                            kb.op("vector", "tensor_scalar", hs.t[:, h * 256:(h + 1) * 256], pout.t[:, 0:256], d.t[:, 2:3], None,
                                  ALU.mult, rd=[pout, d], wr=[hs])
                        else:
                            kb.op("vector", "scalar_tensor_tensor", hs.t[:, h * 256:(h + 1) * 256], pout.t[:, 0:256], d.t[:, 2:3],
                                  hfl[c % 2].t[:, h * 256:(h + 1) * 256], ALU.mult, ALU.add,
                                  rd=[pout, d, hfl[c % 2]], wr=[hs])
                        jl += 1
                    if is_lat and h == 3:
                        if fwd:
                            kb.dma("gpsimd", G["hfd"][b][c - 2], HF[b, c - 2], hsum[c % 2].t[:], rd=[hsum[c % 2]])
                        else:
                            end_chunk_bwd(c)
                if not fwd:
                    flush_bwd()

        for b in bs:
            phase1(b)
            scan(b, 0)
            scan(b, 1)

def host_vecs(inp):
    def ch(v):
        return np.ascontiguousarray(np.asarray(v, np.float32).reshape(-1, 128).T)
    cols = []
    for l in range(2):
        cols.append(ch(inp["ada_b"][l]))
    for l in range(2):
        for j in range(3):
            cols.append(ch(inp["ln_g"][l, j]))
    for l in range(2):
        for j in range(3):
            cols.append(ch(inp["ln_b"][l, j]))
    cols.append(ch(inp["att_q_gain"][0]))
    cols.append(ch(inp["att_k_gain"][0]))
    for tap in range(3):
        cols.append(ch(inp["ml_conv_w"][0, tap]))
    cols.append(ch(inp["ml_conv_b"][0]))
    v = np.concatenate(cols, axis=1)
    assert v.shape == (128, NVEC), v.shape
    return np.ascontiguousarray(v)


def host_bcast(inp):
    ngb = np.ascontiguousarray(np.broadcast_to(np.asarray(inp["ml_norm_g"][0], np.float32)[None, :], (128, D)))
    gbb = np.ascontiguousarray(np.broadcast_to(np.asarray(inp["ml_gate_b"][0], np.float32)[None, :], (128, 16)))
    return {"ngb": ngb, "gbb": gbb}


def host_core_inputs(inp, core):
    bs = slice(core * NB, (core + 1) * NB)
    x = np.asarray(inp["x"][bs], np.float32)
    ctx = np.asarray(inp["ctx"][bs], np.float32)
    hT0 = np.empty((D, TOK), np.float32)
    for b in range(NB):
        hT0[:, ctx0(b):ctx0(b) + NCTX] = ctx[b].T
        hT0[:, lat0(b):lat0(b) + NLAT] = x[b].T
    c = np.asarray(inp["c"][bs], np.float32)
    cc = np.concatenate([c, np.asarray(inp["c_ctx"], np.float32)[None, :], np.zeros((1, D), np.float32)], axis=0)
    cT = np.ascontiguousarray(cc.T.reshape(8, 128, 6).transpose(1, 0, 2))
    return {"hT0": hT0, "cT": cT}


WEIGHT_NAMES = ["ada_w", "ffn_w_in", "ffn_w_out", "att_w_in", "att_w_out", "ml_w_in", "ml_w_out"]


def declare_io(nc, debug_outs=(), need=None):
    io = {}
    shapes = {
        "hT0": [D, TOK], "cT": [128, 8, 6], "vecs": [128, NVEC],
        "ada_w": [2, D, 9 * D], "ffn_w_in": [2, 2, D, 2 * DFF], "ffn_w_out": [2, 2, DFF, D],
        "att_w_in": [1, D, 1536], "att_w_out": [1, D, D], "ml_w_in": [1, D, 3088], "ml_w_out": [1, D, D],
        "rope": [128, 2, NLAT], "cmat": [128, 8, 128], "ngb": [128, D], "gbb": [128, 16],
    }
    if need is not None:
        shapes = {n: s for n, s in shapes.items() if n in need}
    for n, s in shapes.items():
        io[n] = nc.dram_tensor(n, s, F32, kind="ExternalInput").ap()
    for n in ("H1", "H2", "H3", "H4", "H5"):
        kind = "ExternalOutput" if n in debug_outs else "Internal"
        io[n] = nc.dram_tensor(n, [D, TOK], F32, kind=kind).ap()
    io["OUT"] = nc.dram_tensor("OUT", [D, NB * NLAT], F32, kind="ExternalOutput").ap()
    io["HF"] = nc.dram_tensor("HF", [NB, 16, 128, D], F32).ap()
    io["SIG"] = nc.dram_tensor("SIG", [NB, 16, 128, D], F32).ap()
    return io


def alloc_globals(kb):
    G = {}
    G["modsT"] = kb.sb("modsT", [128, 2, 72, 6], F32)
    G["vecs"] = kb.sb("vecs", [128, NVEC], F32)
    G["ones_ln"] = kb.sb("ones_ln", [128, 128], F32)
    G["hfd"] = [[Buf() for _ in range(16)] for _ in range(NB)]
    G["sigd"] = [[Buf() for _ in range(16)] for _ in range(NB)]
    return G


def build_program(nbs=NB, debug_outs=()):
    nc = bass.Bass("TRN2", target_bir_lowering=False)
    io = declare_io(nc, debug_outs=debug_outs)
    blocks = all_blocks()
    lat_blocks = [bk for bk in blocks if bk[1] != 4]
    kb = KB(nc)
    with kb:
        G = alloc_globals(kb)
        prologue(kb, io, G)
        ffn_stage(kb, io, G, io["hT0"], io["H1"], 0, 0, blocks)
        attn_stage(kb, io, G, io["H1"], io["H2"])
        ffn_stage(kb, io, G, io["H2"], io["H3"], 0, 1, blocks)
        ffn_stage(kb, io, G, io["H3"], io["H4"], 1, 0, blocks)
        mlstm_stage(kb, io, G, io["H4"], io["H5"])
        ffn_stage(kb, io, G, io["H5"], io["OUT"], 1, 1, lat_blocks, dst_off=-NB * NCTX, final=True)
        kb.finish()
    return nc, kb


def host_inputs_for_core(inputs, core, shared):
    m = host_core_inputs(inputs, core)
    m.update(shared)
    return m


def host_shared(inputs):
    sh = {"vecs": host_vecs(inputs)}
    sh.update(host_consts())
    sh.update(host_bcast(inputs))
    for n in WEIGHT_NAMES:
        sh[n] = np.ascontiguousarray(np.asarray(inputs[n], np.float32))
    return sh


def gather_out(o):
    return np.ascontiguousarray(o.reshape(D, NB, NLAT).transpose(1, 2, 0))


def kernel(**inputs):
    nc, _ = build_program()
    shared = host_shared(inputs)
    in_maps = [host_inputs_for_core(inputs, c, shared) for c in range(NCORE)]
    res = run_bass_kernel_spmd(nc, in_maps, core_ids=list(range(NCORE)))
    out = np.empty((NCORE * NB, NLAT, D), np.float32)
    for c in range(NCORE):
        out[c * NB:(c + 1) * NB] = gather_out(np.asarray(res.results[c]["OUT"]))
    return out
```
